# Optimizing a Trainium2 kernel written in Bass

```python
import math
import jax
import jax.numpy as jnp
from jax import lax
import numpy as np

D_MODEL = 2048
BATCH = 8
SEQ = 2048
DEPTH = 2

GRID_W = 64
CTX_LEN = 256
N_EVEN = (DEPTH + 1) // 2
N_ODD = DEPTH // 2

DIFF_QK_DIM = 64
DIFF_V_DIM = 2 * DIFF_QK_DIM
FOURIER_WIDTH = D_MODEL // 4
FOURIER_GROUP_DIM = 128
FOURIER_GROUPS = FOURIER_WIDTH // FOURIER_GROUP_DIM
DIFF_WIDTH = D_MODEL - FOURIER_WIDTH
DIFF_HEADS = DIFF_WIDTH // DIFF_V_DIM
EVEN_IN_WIDTH = 3 * DIFF_WIDTH + FOURIER_WIDTH
ROPE_AXIS_DIM = DIFF_QK_DIM // 2
ROPE_BASE = 10000.0
Q_BLOCK = 128

HGRN_EXPAND = 128
HGRN_HEADS = D_MODEL // HGRN_EXPAND
HGRN_HEAD_V = D_MODEL // HGRN_HEADS
FORGET_DIM = HGRN_HEADS * HGRN_EXPAND
HGRN_IN_WIDTH = 3 * FORGET_DIM + 2 * D_MODEL
HGRN_CHUNK = 32

D_FF = 5504
CONV_W = 3

ALPHA = (2.0 * DEPTH) ** 0.25
BETA = (8.0 * DEPTH) ** -0.25
LN_EPS = 1e-6
RMS_EPS = 1e-5
MOD_INIT = 0.5

kernel_name = 'hybrid_diffattn_fnet_hgrn2_dit'


def layer_norm(t, g=None, b=None):
    t32 = t.astype(jnp.float32)
    mu = jnp.mean(t32, axis=-1, keepdims=True)
    var = jnp.mean(jnp.square(t32 - mu), axis=-1, keepdims=True)
    y = (t32 - mu) * lax.rsqrt(var + LN_EPS)
    if g is not None:
        y = y * g.astype(jnp.float32) + b.astype(jnp.float32)
    return y.astype(t.dtype)


def rms_norm(t, w):
    t32 = t.astype(jnp.float32)
    y = t32 * lax.rsqrt(jnp.mean(jnp.square(t32), axis=-1, keepdims=True) + RMS_EPS) * w.astype(jnp.float32)
    return y.astype(t.dtype)


def modulate(t, shift, scale):
    return t * (1 + scale) + shift


def residual_post_norm(h, update, g, b):
    return layer_norm(ALPHA * h + update, g, b)


def to_heads(t, hd):
    b, n, _ = t.shape
    return t.reshape(b, n, -1, hd).transpose(0, 2, 1, 3)


def from_heads(t):
    b, h, n, d = t.shape
    return t.transpose(0, 2, 1, 3).reshape(b, n, h * d)


def rope_tables(ids):
    inv = 1.0 / (ROPE_BASE ** (jnp.arange(0, ROPE_AXIS_DIM, 2, dtype=jnp.float32) / ROPE_AXIS_DIM))
    ang = ids.astype(jnp.float32)[:, None] * inv[None, :]
    return jnp.cos(ang), jnp.sin(ang)


def rotate(t, cos, sin):
    half = t.shape[-1] // 2
    t1, t2 = t[..., :half], t[..., half:]
    cos = cos.astype(t.dtype)
    sin = sin.astype(t.dtype)
    return jnp.concatenate([t1 * cos - t2 * sin, t2 * cos + t1 * sin], axis=-1)


def axial_rope(t, rope):
    cos_r, sin_r, cos_c, sin_c = rope
    return jnp.concatenate([rotate(t[..., :ROPE_AXIS_DIM], cos_r, sin_r),
                            rotate(t[..., ROPE_AXIS_DIM:], cos_c, sin_c)], axis=-1)


def diff_attend(q1, q2, k1, k2, v, lam):
    scale = DIFF_QK_DIM ** -0.5
    s1 = jnp.einsum('bhqd,bhkd->bhqk', q1, k1).astype(jnp.float32) * scale
    s2 = jnp.einsum('bhqd,bhkd->bhqk', q2, k2).astype(jnp.float32) * scale
    p = jax.nn.softmax(s1, axis=-1) - lam * jax.nn.softmax(s2, axis=-1)
    return jnp.einsum('bhqk,bhkd->bhqd', p.astype(v.dtype), v)


def diff_attend_blocked(q1, q2, k1, k2, v, lam):
    b, h, n, d = q1.shape
    nb = n // Q_BLOCK
    blocks = lambda t: t.reshape(b, h, nb, Q_BLOCK, d).transpose(2, 0, 1, 3, 4)
    out = lax.map(lambda qs: diff_attend(qs[0], qs[1], k1, k2, v, lam), (blocks(q1), blocks(q2)))
    return out.transpose(1, 2, 0, 3, 4).reshape(b, h, n, v.shape[-1])


def fourier_mix(u):
    b, n, _ = u.shape
    g = u.reshape(b, n, FOURIER_GROUPS, FOURIER_GROUP_DIM).astype(jnp.float32)
    y = jnp.fft.fft2(g, axes=(1, 3), norm='ortho').real
    return y.reshape(b, n, FOURIER_WIDTH).astype(u.dtype)


def even_mixer(u_lat, u_ctx, rope, w_in, w_out, lam_vecs, subln_w, lam_init, need_ctx):
    splits = [DIFF_WIDTH, 2 * DIFF_WIDTH, 3 * DIFF_WIDTH]
    q_l, k_l, v_l, f_l = jnp.split(u_lat @ w_in, splits, axis=-1)
    q_c, k_c, v_c, f_c = jnp.split(u_ctx @ w_in, splits, axis=-1)
    lv = lam_vecs.astype(jnp.float32)
    lam = jnp.exp(jnp.sum(lv[0] * lv[1])) - jnp.exp(jnp.sum(lv[2] * lv[3])) + lam_init

    def pair(t, use_rope):
        th = to_heads(t, DIFF_V_DIM)
        a, b = th[..., :DIFF_QK_DIM], th[..., DIFF_QK_DIM:]
        if use_rope:
            a, b = axial_rope(a, rope), axial_rope(b, rope)
        return a, b

    q1, q2 = pair(q_l, True)
    k1, k2 = pair(k_l, True)
    kc1, kc2 = pair(k_c, False)
    v = to_heads(v_l, DIFF_V_DIM)
    vc = to_heads(v_c, DIFF_V_DIM)

    def merge(attn, four_in):
        attn = rms_norm(attn, subln_w) * (1.0 - lam_init)
        return jnp.concatenate([from_heads(attn), fourier_mix(four_in)], axis=-1) @ w_out

    keys1 = jnp.concatenate([kc1, k1], axis=2)
    keys2 = jnp.concatenate([kc2, k2], axis=2)
    vals = jnp.concatenate([vc, v], axis=2)
    m_lat = merge(diff_attend_blocked(q1, q2, keys1, keys2, vals, lam), f_l)
    m_ctx = None
    if need_ctx:
        qc1, qc2 = pair(q_c, False)
        m_ctx = merge(diff_attend(qc1, qc2, kc1, kc2, vc, lam), f_c)
    return m_lat, m_ctx


def gla_chunked(q, k, v, logf, state):
    b, n, h, dk = q.shape
    dv = v.shape[-1]
    nc = n // HGRN_CHUNK
    to_chunks = lambda t: t.reshape(b, nc, HGRN_CHUNK, h, t.shape[-1]).transpose(1, 0, 3, 2, 4)
    mask = jnp.tril(jnp.ones((HGRN_CHUNK, HGRN_CHUNK), dtype=bool))

    def step(s, inp):
        qc, kc, vc, gc = inp
        cum = jnp.cumsum(gc, axis=2)
        o_inter = jnp.einsum('bhtk,bhkv->bhtv', qc * jnp.exp(cum), s)
        rel = cum[:, :, :, None, :] - cum[:, :, None, :, :]
        decay = jnp.exp(jnp.where(mask[:, :, None], rel, -jnp.inf))
        scores = jnp.einsum('bhtk,bhsk,bhtsk->bhts', qc, kc, decay)
        o_intra = jnp.einsum('bhts,bhsv->bhtv', scores, vc)
        cum_end = cum[:, :, -1:, :]
        s = jnp.exp(cum_end[:, :, 0, :])[..., None] * s + jnp.einsum('bhsk,bhsv->bhkv', kc * jnp.exp(cum_end - cum), vc)
        return s, o_inter + o_intra

    s_final, o = lax.scan(step, state, (to_chunks(q), to_chunks(k), to_chunks(v), to_chunks(logf)))
    return o.transpose(1, 0, 3, 2, 4).reshape(b, n, h, dv), s_final


def hgrn_mixer(u_lat, u_ctx, w_in, w_out, lb, norm_w, need_ctx):
    splits = [FORGET_DIM, 2 * FORGET_DIM, 3 * FORGET_DIM, 3 * FORGET_DIM + D_MODEL]

    def prep(u):
        b, n, _ = u.shape
        q, f_fwd, f_bwd, i, g = jnp.split(u @ w_in, splits, axis=-1)
        hd = lambda t, d: t.reshape(b, n, HGRN_HEADS, d).astype(jnp.float32)
        q = jax.nn.silu(hd(q, HGRN_EXPAND))
        v = hd(i, HGRN_HEAD_V)
        gates = []
        for d, f_pre in enumerate((f_fwd, f_bwd)):
            f_pre = hd(f_pre, HGRN_EXPAND)
            lb_d = lb[d].reshape(HGRN_HEADS, HGRN_EXPAND)
            logf = jnp.log(lb_d + (1.0 - lb_d) * jax.nn.sigmoid(f_pre))
            k = (1.0 - lb_d) * jax.nn.sigmoid(-f_pre)
            gates.append((k, logf))
        return q, v, gates, g

    qc, vc, gates_c, g_c = prep(u_ctx)
    ql, vl, gates_l, g_l = prep(u_lat)
    zero = jnp.zeros((u_ctx.shape[0], HGRN_HEADS, HGRN_EXPAND, HGRN_HEAD_V), jnp.float32)
    outs_lat, outs_ctx = [], []
    for d in range(2):
        flip = (lambda t: jnp.flip(t, axis=1)) if d == 1 else (lambda t: t)
        (kc, lfc), (kl, lfl) = gates_c[d], gates_l[d]
        oc, s_ctx = gla_chunked(flip(qc), flip(kc), flip(vc), flip(lfc), zero)
        ol, _ = gla_chunked(flip(ql), flip(kl), flip(vl), flip(lfl), s_ctx)
        outs_lat.append(flip(ol))
        outs_ctx.append(flip(oc))

    def readout(o, g, u):
        b, n, _ = u.shape
        gate = jax.nn.silu(g.reshape(b, n, HGRN_HEADS, HGRN_HEAD_V).astype(jnp.float32))
        y = rms_norm(o, norm_w) * gate
        return y.reshape(b, n, D_MODEL).astype(u.dtype) @ w_out

    m_lat = readout(outs_lat[0] + outs_lat[1], g_l, u_lat)
    m_ctx = readout(outs_ctx[0] + outs_ctx[1], g_c, u_ctx) if need_ctx else None
    return m_lat, m_ctx


def conv_ffn(u, w_up, conv_w, conv_b, w_down):
    a, v = jnp.split(u @ w_up, 2, axis=-1)
    n = a.shape[1]
    pad = CONV_W // 2
    a_pad = jnp.pad(a, ((0, 0), (pad, pad), (0, 0)))
    conv = conv_b
    for tap in range(CONV_W):
        conv = conv + a_pad[:, tap:tap + n] * conv_w[tap]
    return (jax.nn.gelu(conv, approximate=False) * v) @ w_down


def setup_inputs(seed: int = 0) -> dict:
    key = jax.random.key(seed)
    ks = jax.random.split(key, 22)
    d = D_MODEL
    nrm = lambda k, shape, scale: jax.random.normal(k, shape, jnp.float32) * scale
    return {
        'x': nrm(ks[0], (BATCH, SEQ, d), 1.0),
        'c': nrm(ks[1], (BATCH, d), 1.0),
        'ctx': nrm(ks[2], (BATCH, CTX_LEN, d), 1.0),
        'c_ctx': nrm(ks[3], (d,), 1.0),
        'mod_w': nrm(ks[4], (DEPTH, d, 6 * d), MOD_INIT * d ** -0.5),
        'mod_b': nrm(ks[5], (DEPTH, 6 * d), 0.02),
        'ln_mix_g': 1.0 + nrm(ks[6], (DEPTH, d), 0.02),
        'ln_mix_b': nrm(ks[7], (DEPTH, d), 0.02),
        'ln_ffn_g': 1.0 + nrm(ks[8], (DEPTH, d), 0.02),
        'ln_ffn_b': nrm(ks[9], (DEPTH, d), 0.02),
        'even_w_in': nrm(ks[10], (N_EVEN, d, EVEN_IN_WIDTH), d ** -0.5),
        'even_w_out': nrm(ks[11], (N_EVEN, DIFF_WIDTH + FOURIER_WIDTH, d), BETA * (DIFF_WIDTH + FOURIER_WIDTH) ** -0.5),
        'diff_lambda': nrm(ks[12], (N_EVEN, 4, DIFF_QK_DIM), 0.1),
        'diff_subln': 1.0 + nrm(ks[13], (N_EVEN, DIFF_V_DIM), 0.02),
        'hgrn_w_in': nrm(ks[14], (N_ODD, d, HGRN_IN_WIDTH), d ** -0.5),
        'hgrn_w_out': nrm(ks[15], (N_ODD, d, d), BETA * d ** -0.5),
        'hgrn_lower_bounds': nrm(ks[16], (2, DEPTH, FORGET_DIM), 1.0),
        'hgrn_norm': 1.0 + nrm(ks[17], (N_ODD, HGRN_HEAD_V), 0.02),
        'ffn_w_up': nrm(ks[18], (DEPTH, d, 2 * D_FF), d ** -0.5),
        'ffn_conv_w': nrm(ks[19], (DEPTH, CONV_W, D_FF), CONV_W ** -0.5),
        'ffn_conv_b': nrm(ks[20], (DEPTH, D_FF), 0.02),
        'ffn_w_down': nrm(ks[21], (DEPTH, D_FF, d), BETA * D_FF ** -0.5),
    }


def reference(x, c, ctx, c_ctx, mod_w, mod_b, ln_mix_g, ln_mix_b, ln_ffn_g, ln_ffn_b,
              even_w_in, even_w_out, diff_lambda, diff_subln,
              hgrn_w_in, hgrn_w_out, hgrn_lower_bounds, hgrn_norm,
              ffn_w_up, ffn_conv_w, ffn_conv_b, ffn_w_down):
    n = x.shape[1]
    rows = n // GRID_W
    row_ids = jnp.repeat(jnp.arange(rows), GRID_W)
    col_ids = jnp.tile(jnp.arange(GRID_W), rows)
    rope = rope_tables(row_ids) + rope_tables(col_ids)

    lb_all = jnp.cumsum(jax.nn.softmax(hgrn_lower_bounds.astype(jnp.float32), axis=1), axis=1)
    lb_all = lb_all - lb_all[:, :1]

    silu_c = jax.nn.silu(c)
    silu_cc = jax.nn.silu(c_ctx)
    h_lat, h_ctx = x, ctx
    for layer in range(DEPTH):
        last = layer == DEPTH - 1
        slot = layer // 2
        mod_l = (silu_c @ mod_w[layer] + mod_b[layer])[:, None, :]
        mod_c = (silu_cc @ mod_w[layer] + mod_b[layer])[None, None, :]
        sh_a, sc_a, gt_a, sh_f, sc_f, gt_f = jnp.split(mod_l, 6, axis=-1)
        csh_a, csc_a, cgt_a, csh_f, csc_f, cgt_f = jnp.split(mod_c, 6, axis=-1)
        u_lat = modulate(layer_norm(h_lat), sh_a, sc_a)
        u_ctx = modulate(layer_norm(h_ctx), csh_a, csc_a)
        if layer % 2 == 0:
            lam_init = 0.8 - 0.6 * math.exp(-0.3 * layer)
            m_lat, m_ctx = even_mixer(u_lat, u_ctx, rope, even_w_in[slot], even_w_out[slot],
                                      diff_lambda[slot], diff_subln[slot], lam_init, not last)
        else:
            m_lat, m_ctx = hgrn_mixer(u_lat, u_ctx, hgrn_w_in[slot], hgrn_w_out[slot],
                                      lb_all[:, layer], hgrn_norm[slot], not last)
        h_lat = residual_post_norm(h_lat, gt_a * m_lat, ln_mix_g[layer], ln_mix_b[layer])
        f_lat = conv_ffn(modulate(layer_norm(h_lat), sh_f, sc_f),
                         ffn_w_up[layer], ffn_conv_w[layer], ffn_conv_b[layer], ffn_w_down[layer])
        h_lat = residual_post_norm(h_lat, gt_f * f_lat, ln_ffn_g[layer], ln_ffn_b[layer])
        if not last:
            h_ctx = residual_post_norm(h_ctx, cgt_a * m_ctx, ln_mix_g[layer], ln_mix_b[layer])
            f_ctx = conv_ffn(modulate(layer_norm(h_ctx), csh_f, csc_f),
                             ffn_w_up[layer], ffn_conv_w[layer], ffn_conv_b[layer], ffn_w_down[layer])
            h_ctx = residual_post_norm(h_ctx, cgt_f * f_ctx, ln_ffn_g[layer], ln_ffn_b[layer])
    return h_lat
```

```python
import contextlib
import math
import numpy as np
import ml_dtypes
import concourse.bass as bass
import concourse.mybir as mybir
from concourse.bass_utils import run_bass_kernel_spmd

F32 = mybir.dt.float32
BF16 = mybir.dt.bfloat16
AF = mybir.ActivationFunctionType
ALU = mybir.AluOpType
AX = mybir.AxisListType

D = 2048
NCTX = 256
NLAT = 2048
T = NCTX + NLAT
NT = T // 128
KC = D // 128
DFF = 5504
FC = DFF // 128
ALPHA = (2.0 * 2) ** 0.25
LN_EPS = 1e-6
RMS_EPS = 1e-5
SELF_SYNC = True


class TokSet:
    __slots__ = ("d",)

    def __init__(self):
        self.d = {}

    def add(self, tok):
        if tok is None:
            return
        sem, val = tok
        k = id(sem)
        cur = self.d.get(k)
        if cur is None or cur[1] < val:
            self.d[k] = (sem, val)

    def update(self, other):
        for t in other.d.values():
            self.add(t)

    def toks(self):
        return list(self.d.values())


class Buf:
    def __init__(self, name=""):
        self.name = name
        self.writers = TokSet()
        self.readers = TokSet()
        self.war = TokSet()
        self.dsem = None

    def begin(self):
        w = TokSet()
        w.update(self.writers)
        w.update(self.readers)
        self.war = w
        self.writers = TokSet()
        self.readers = TokSet()


class Eng:
    def __init__(self, k, name, eng):
        self.k = k
        self.name = name
        self.eng = eng
        self.sem = None
        self.count = 0
        self.waited = {}
        self.pending_unsignaled = False

    def wait(self, tok):
        sem, val = tok
        if sem is self.sem:
            if self.name == "pe" or not SELF_SYNC:
                return
        k = id(sem)
        if self.waited.get(k, 0) >= val:
            return
        self.waited[k] = val
        self.eng.wait_ge(sem, val)


class KB:
    def __init__(self, nc):
        self.nc = nc
        self.pe = Eng(self, "pe", nc.tensor)
        self.act = Eng(self, "act", nc.scalar)
        self.dve = Eng(self, "dve", nc.vector)
        self.pool = Eng(self, "pool", nc.gpsimd)
        self.sp = Eng(self, "sp", nc.sync)
        self.engs = [self.pe, self.act, self.dve, self.pool, self.sp]
        self.stack = contextlib.ExitStack()
        self.dma_sems = []
        self.dma_free = []
        self.phase_bufs = []
        self.nsem = 0

    def new_sem(self, name):
        self.nsem += 1
        return self.stack.enter_context(self.nc.semaphore(f"{name}_{self.nsem}"))

    def begin_batch(self):
        self.slot = 0

    def start_phase(self):
        if not hasattr(self, "slot"):
            self.slot = 0
            self.slot_sems = {}
        if not hasattr(self, "slot_sems"):
            self.slot_sems = {}
        self.cur_slot = self.slot
        self.slot += 1
        saved = self.slot_sems.get(self.cur_slot)
        for e in self.engs:
            if e.name == "sp":
                continue
            if saved is None:
                e.sem = self.new_sem("e" + e.name)
                e.count = 0
            else:
                e.sem, e.count = saved[e.name]

    def get_dsem(self, buf):
        if buf.dsem is None:
            if self.dma_free:
                buf.dsem = self.dma_free.pop()
            else:
                buf.dsem = [self.new_sem("d"), 0]
                self.dma_sems.append(buf.dsem)
            self.phase_bufs.append(buf)
        return buf.dsem

    def barrier(self):
        toks = []
        for e in self.engs:
            if e.sem is not None and e.count > 0:
                toks.append((e.sem, e.count))
        for ds in self.dma_sems:
            if ds[1] > 0:
                toks.append((ds[0], ds[1]))
        for e in self.engs:
            for t in toks:
                if t[0] is e.sem:
                    continue
                e.wait(t)

    def end_phase(self):
        self.barrier()
        self.slot_sems[self.cur_slot] = {e.name: (e.sem, e.count) for e in self.engs if e.name != "sp"}
        for b in self.phase_bufs:
            if b.dsem is not None:
                self.dma_free.append(b.dsem)
                b.dsem = None
        self.phase_bufs = []

    def _deps(self, e, reads, writes, begins, deps):
        ts = TokSet()
        for b in begins:
            b.begin()
            ts.update(b.war)
        for b in writes:
            ts.update(b.war)
            ts.update(b.readers)
        for b in reads:
            ts.update(b.writers)
        for t in deps:
            ts.add(t)
        for t in ts.toks():
            e.wait(t)

    def op(self, e, fn, *args, reads=(), writes=(), begins=(), deps=(), signal=True, **kw):
        self._deps(e, reads, writes, begins, deps)
        ins = fn(*args, **kw)
        if signal:
            e.count += 1
            ins.then_inc(e.sem, 1)
            tok = (e.sem, e.count)
        else:
            tok = (e.sem, e.count + 1)
        for b in reads:
            b.readers.add(tok)
        for b in writes:
            b.writers.add(tok)
        for b in begins:
            b.writers.add(tok)
        return tok

    def dma(self, e, out, in_, sbuf_buf, reads=(), writes=(), begins=(), deps=(), **kw):
        self._deps(e, reads, writes, begins, deps)
        ds = self.get_dsem(sbuf_buf)
        ds[1] += 16
        e.eng.dma_start(out=out, in_=in_, **kw).then_inc(ds[0], 16)
        tok = (ds[0], ds[1])
        for b in reads:
            b.readers.add(tok)
        for b in writes:
            b.writers.add(tok)
        for b in begins:
            b.writers.add(tok)
        return tok


class Pool:
    def __init__(self, kb, es, name, shape, dtype, n, psum=False):
        self.tiles = []
        for i in range(n):
            if psum:
                t = es.enter_context(kb.nc.psum_tensor(uname(f"{name}{i}"), shape, dtype))
            else:
                t = es.enter_context(kb.nc.sbuf_tensor(uname(f"{name}{i}"), shape, dtype))
            self.tiles.append((t, Buf(f"{name}{i}")))
        self.i = 0

    def next(self):
        r = self.tiles[self.i % len(self.tiles)]
        self.i += 1
        return r


_UID = [0]


def uname(name):
    _UID[0] += 1
    return f"{name}_u{_UID[0]}"


def sb(kb, es, name, shape, dtype):
    t = es.enter_context(kb.nc.sbuf_tensor(uname(name), shape, dtype))
    return t, Buf(name)


class Res:
    pass


def make_consts(kb, es):
    nc = kb.nc
    r = Res()
    r.identb, r.identb_b = sb(kb, es, "identb", [128, 128], BF16)
    r.identf, r.identf_b = sb(kb, es, "identf", [128, 128], F32)
    r.eps, r.eps_b = sb(kb, es, "epsln", [128, 1], F32)
    r.epsr, r.epsr_b = sb(kb, es, "epsrms", [128, 1], F32)
    r.onesb, r.onesb_b = sb(kb, es, "onesb", [128, 128], BF16)
    r.onesf, r.onesf_b = sb(kb, es, "onesf", [128, 128], F32)
    kb.op(kb.pool, nc.gpsimd.memset, r.identb[:], 1.0, begins=[r.identb_b])
    kb.op(kb.pool, nc.gpsimd.affine_select, out=r.identb[:], in_=r.identb[:], pattern=[[-1, 128]],
          compare_op=ALU.is_equal, fill=0.0, base=0, channel_multiplier=1,
          reads=[r.identb_b], writes=[r.identb_b])
    kb.op(kb.pool, nc.gpsimd.memset, r.identf[:], 1.0, begins=[r.identf_b])
    kb.op(kb.pool, nc.gpsimd.affine_select, out=r.identf[:], in_=r.identf[:], pattern=[[-1, 128]],
          compare_op=ALU.is_equal, fill=0.0, base=0, channel_multiplier=1,
          reads=[r.identf_b], writes=[r.identf_b])
    kb.op(kb.pool, nc.gpsimd.memset, r.eps[:], LN_EPS, begins=[r.eps_b])
    kb.op(kb.pool, nc.gpsimd.memset, r.epsr[:], RMS_EPS, begins=[r.epsr_b])
    kb.op(kb.pool, nc.gpsimd.memset, r.onesb[:], 1.0, begins=[r.onesb_b])
    kb.op(kb.pool, nc.gpsimd.memset, r.onesf[:], 1.0, begins=[r.onesf_b])
    return r


def load_colT(kb, es, cs, name, row_ap, n, pst):
    nc = kb.nc
    rt, rtb = sb(kb, es, name + "r", [n, 128], F32)
    kb.dma(kb.sp, out=rt[:], in_=row_ap.rearrange("(j p) -> j p", p=128), sbuf_buf=rtb, begins=[rtb])
    t, tb = sb(kb, es, name, [128, n], F32)
    ps, psb = pst.next()
    kb.op(kb.pe, nc.tensor.transpose, out=ps[:, 0:n], in_=rt[:, :], identity=cs.identf[0:n, 0:n],
          reads=[rtb, cs.identf_b], begins=[psb])
    kb.op(kb.dve, nc.vector.tensor_copy, out=t[:], in_=ps[:, 0:n], reads=[psb], begins=[tb])
    return t, tb


def load_modT(kb, es, cs, dr, l, name, pst):
    nc = kb.nc
    outs = []
    for r in range(2):
        m, mb = load_colT(kb, es, cs, f"{name}{r}", dr["modrow"][l, r], 96, pst)
        for j0 in (16, 64):
            kb.op(kb.dve, nc.vector.tensor_scalar, out=m[:, j0:j0 + 16], in0=m[:, j0:j0 + 16], scalar1=1.0,
                  scalar2=None, op0=ALU.add, reads=[mb], writes=[mb])
        outs.append((m, mb))
    return outs


def load_bcast(kb, es, name, row_ap, n, eng=None):
    t, b = sb(kb, es, name, [128, n], F32)
    kb.dma(eng or kb.sp, out=t[:], in_=row_ap.partition_broadcast(128), sbuf_buf=b, begins=[b])
    return t, b


class LNRes:
    def __init__(self, kb, es, tag=""):
        self.stats = Pool(kb, es, "lnst" + tag, [128, 4, 6], F32, 2)
        self.mv = Pool(kb, es, "lnmv" + tag, [128, 2], F32, 2)
        self.rstd = Pool(kb, es, "lnrs" + tag, [128, 1], F32, 2)
        self.nmr = Pool(kb, es, "lnnm" + tag, [128, 1], F32, 2)


def emit_ln_stats(kb, cs, lr, x, xb):
    nc = kb.nc
    st, stb = lr.stats.next()
    for j in range(4):
        kb.op(kb.dve, nc.vector.bn_stats, out=st[:, j, :], in_=x[:, j * 512:(j + 1) * 512], reads=[xb],
              begins=[stb] if j == 0 else (), writes=[stb] if j else ())
    mv, mvb = lr.mv.next()
    kb.op(kb.dve, nc.vector.bn_aggr, out=mv[:], in_=st[:], reads=[stb], begins=[mvb])
    rs, rsb = lr.rstd.next()
    kb.op(kb.act, nc.scalar.activation, out=rs[:], in_=mv[:, 1:2], func=AF.Sqrt, bias=cs.eps[:], scale=1.0,
          reads=[mvb, cs.eps_b], begins=[rsb])
    kb.op(kb.dve, nc.vector.reciprocal, out=rs[:], in_=rs[:], reads=[rsb], writes=[rsb])
    nm, nmb = lr.nmr.next()
    kb.op(kb.dve, nc.vector.tensor_scalar, out=nm[:], in0=mv[:, 0:1], scalar1=rs[:, 0:1], scalar2=-1.0,
          op0=ALU.mult, op1=ALU.mult, reads=[mvb, rsb], begins=[nmb])
    return rs, rsb, nm, nmb


def emit_ln_to_uT(kb, cs, lr, x, xb, nbpool, pstp, scaleT, shiftT, modb, uT, uTb, col0, first_write):
    nc = kb.nc
    rs, rsb, nm, nmb = emit_ln_stats(kb, cs, lr, x, xb)
    nb, nbb = nbpool.next()
    kb.op(kb.act, nc.scalar.activation, out=nb[:], in_=x[:], func=AF.Identity, scale=rs[:, 0:1], bias=nm[:, 0:1],
          reads=[xb, rsb, nmb], begins=[nbb])
    for q in range(4):
        pt, ptb = pstp.next()
        for j in range(4):
            kc = q * 4 + j
            kb.op(kb.pe, nc.tensor.transpose, out=pt[:, j * 128:(j + 1) * 128], in_=nb[:, kc * 128:(kc + 1) * 128],
                  identity=cs.identb[:], reads=[nbb, cs.identb_b], begins=[ptb] if j == 0 else (),
                  writes=[ptb] if j else (), signal=(j == 3))
        for j in range(4):
            kc = q * 4 + j
            beg = first_write and kc == 0
            if q % 2 == 0:
                kb.op(kb.act, nc.scalar.activation, out=uT[:, kc, col0:col0 + 128], in_=pt[:, j * 128:(j + 1) * 128],
                      func=AF.Identity, scale=scaleT[:, kc:kc + 1], bias=shiftT[:, kc:kc + 1],
                      reads=[ptb, modb], begins=[uTb] if beg else (), writes=() if beg else [uTb])
            else:
                kb.op(kb.dve, nc.vector.tensor_scalar, out=uT[:, kc, col0:col0 + 128], in0=pt[:, j * 128:(j + 1) * 128],
                      scalar1=scaleT[:, kc:kc + 1], scalar2=shiftT[:, kc:kc + 1], op0=ALU.mult, op1=ALU.add,
                      reads=[ptb, modb], writes=[uTb])


def phase_mod(kb, dr):
    nc = kb.nc
    with contextlib.ExitStack() as es:
        kb.start_phase()
        cT, cTb = sb(kb, es, "cT", [128, 2, 16], F32)
        sT, sTb = sb(kb, es, "sT", [128, 16, 2], BF16)
        for r in range(2):
            kb.dma(kb.sp, out=cT[:, r, :], in_=dr["c2"][r].rearrange("(kc p) -> p kc", p=128), sbuf_buf=cTb,
                   begins=[cTb] if r == 0 else (), writes=[cTb] if r else (), allow_slow_non_contiguous=True)
        for r in range(2):
            kb.op(kb.act, nc.scalar.activation, out=sT[:, :, r], in_=cT[:, r, :], func=AF.Silu, reads=[cTb],
                  begins=[sTb] if r == 0 else (), writes=[sTb] if r else ())
        wpool = Pool(kb, es, "mw", [128, 16, 512], BF16, 3)
        pspool = Pool(kb, es, "mps", [2, 512], F32, 2, psum=True)
        bias, biasb = sb(kb, es, "mbias", [2, 6 * D], F32)
        row, rowb = sb(kb, es, "mrow", [2, 6 * D], F32)
        for l in range(2):
            kb.dma(kb.sp, out=bias[:], in_=dr["mod_b"][l].partition_broadcast(2), sbuf_buf=biasb, begins=[biasb])
            for cg in range(24):
                w, wb = wpool.next()
                kb.dma(kb.pool, out=w[:], in_=dr["mod_w"][l, :, cg * 512:(cg + 1) * 512].rearrange("(kc p) n -> p kc n", p=128),
                       sbuf_buf=wb, begins=[wb])
                ps, psb = pspool.next()
                for kc in range(16):
                    kb.op(kb.pe, nc.tensor.matmul, ps[:], lhsT=sT[:, kc, :], rhs=w[:, kc, :], start=(kc == 0),
                          stop=(kc == 15), reads=[sTb, wb], begins=[psb] if kc == 0 else (),
                          writes=[psb] if kc else (), signal=(kc == 15))
                kb.op(kb.dve, nc.vector.tensor_tensor, out=row[:, cg * 512:(cg + 1) * 512], in0=ps[:],
                      in1=bias[:, cg * 512:(cg + 1) * 512], op=ALU.add, reads=[psb, biasb],
                      begins=[rowb] if cg == 0 else (), writes=[rowb] if cg else ())
            kb.dma(kb.sp, out=dr["modrow"][l], in_=row[:], sbuf_buf=rowb, reads=[rowb])
        kb.end_phase()


UFW = 2308


def ufcol(tok):
    return 1 + tok if tok < NCTX else 259 + (tok - NCTX)


def phase_post(kb, dr, l, cat_ap, wout_ap, hin_fn, tiles, lng, lnb):
    nc = kb.nc
    with contextlib.ExitStack() as es:
        kb.start_phase()
        cs = make_consts(kb, es)
        lr = LNRes(kb, es)
        psm = Pool(kb, es, "psm", [128, 512], F32, 3, psum=True)
        pst = Pool(kb, es, "pst", [128, 1024], BF16, 2, psum=True)
        mods = load_modT(kb, es, cs, dr, l, "modT", psm)
        G, Gb = load_bcast(kb, es, "lnG", lng, D)
        Bt, Bb = load_bcast(kb, es, "lnB", lnb, D)
        gates = {}
        for r in sorted({0 if tt >= 2 else 1 for tt in tiles}):
            gates[r] = load_bcast(kb, es, f"gate{r}", dr["modrow"][l, r, 2 * D:3 * D], D)
        w, wb = sb(kb, es, "wout", [128, KC, D], BF16)
        wv = wout_ap.rearrange("(kc p) n -> p kc n", p=128)
        for q in range(4):
            kb.dma(kb.pool, out=w[:, q * 4:(q + 1) * 4, :], in_=wv[:, q * 4:(q + 1) * 4, :], sbuf_buf=wb,
                   begins=[wb] if q == 0 else (), writes=[wb] if q else ())
        ufv = dr["uf"]
        catp = Pool(kb, es, "catt", [128, KC, 128], BF16, 2)
        hp = Pool(kb, es, "hin", [128, D], F32, 2)
        zp = Pool(kb, es, "zt", [128, D], F32, 2)
        tmpp = Pool(kb, es, "ztmp", [128, 512], F32, 2)
        nbp = Pool(kb, es, "nb", [128, D], BF16, 2)
        ufp = Pool(kb, es, "uft", [128, KC, 128], BF16, 2)
        catv = cat_ap.rearrange("(kc p) t -> p kc t", p=128)
        for tt in tiles:
            r = 0 if tt >= 2 else 1
            gt, gtb = gates[r]
            ct, ctb = catp.next()
            kb.dma(kb.sp, out=ct[:], in_=catv[:, :, tt * 128:(tt + 1) * 128], sbuf_buf=ctb, begins=[ctb])
            h, hb = hp.next()
            kb.dma(kb.sp, out=h[:], in_=hin_fn(tt), sbuf_buf=hb, begins=[hb])
            z, zb = zp.next()
            for cg in range(4):
                ps, psb = psm.next()
                for kc in range(KC):
                    kb.op(kb.pe, nc.tensor.matmul, ps[:], lhsT=ct[:, kc, :], rhs=w[:, kc, cg * 512:(cg + 1) * 512],
                          start=(kc == 0), stop=(kc == KC - 1), reads=[ctb, wb], begins=[psb] if kc == 0 else (),
                          writes=[psb] if kc else (), signal=(kc == KC - 1))
                tm, tmb = tmpp.next()
                kb.op(kb.dve, nc.vector.tensor_tensor, out=tm[:], in0=ps[:], in1=gt[:, cg * 512:(cg + 1) * 512],
                      op=ALU.mult, reads=[psb, gtb], begins=[tmb])
                kb.op(kb.dve, nc.vector.scalar_tensor_tensor, out=z[:, cg * 512:(cg + 1) * 512],
                      in0=h[:, cg * 512:(cg + 1) * 512], scalar=ALPHA, in1=tm[:], op0=ALU.mult, op1=ALU.add,
                      reads=[hb, tmb], begins=[zb] if cg == 0 else (), writes=[zb] if cg else ())
            rs, rsb, nm, nmb = emit_ln_stats(kb, cs, lr, z, zb)
            kb.op(kb.act, nc.scalar.activation, out=z[:], in_=z[:], func=AF.Identity, scale=rs[:, 0:1], bias=nm[:, 0:1],
                  reads=[zb, rsb, nmb], writes=[zb])
            kb.op(kb.pool, nc.gpsimd.tensor_tensor, out=z[:], in0=z[:], in1=G[:], op=ALU.mult, reads=[zb, Gb], writes=[zb])
            kb.op(kb.pool, nc.gpsimd.tensor_tensor, out=z[:], in0=z[:], in1=Bt[:], op=ALU.add, reads=[zb, Bb], writes=[zb])
            kb.dma(kb.sp, out=dr["hmid"][tt * 128:(tt + 1) * 128, :], in_=z[:], sbuf_buf=zb, reads=[zb])
            uft, uftb = ufp.next()
            emit_ln_to_uT(kb, cs, lr, z, zb, nbp, pst, mods[r][0][:, 64:80], mods[r][0][:, 48:64], mods[r][1], uft, uftb, 0, True)
            c0 = ufcol(tt * 128)
            kb.dma(kb.sp, out=ufv[:, :, c0:c0 + 128], in_=uft[:], sbuf_buf=uftb, reads=[uftb])
        kb.end_phase()


def ffn_groups(tiles):
    groups = []
    i = 0
    while i < len(tiles):
        g = tiles[i:i + 6]
        i += 6
        groups.append(g)
    out = []
    for g in groups:
        blocks = []
        runs = []
        for tt in g:
            seq = 0 if tt < 2 else 1
            if runs and runs[-1][0] == seq and runs[-1][2] == tt:
                runs[-1][2] = tt + 1
            else:
                runs.append([seq, tt, tt + 1])
        for seq, a, b in runs:
            ntok = (b - a) * 128
            c = ufcol(a * 128)
            nblk = (ntok + 383) // 384
            per = ntok // nblk
            assert per * nblk == ntok
            for j in range(nblk):
                blocks.append((c + j * per - 1, per + 2, (a - g[0]) * 128 + j * per))
        lo = min(bk[0] for bk in blocks)
        hi = max(bk[0] + bk[1] for bk in blocks)
        out.append((g, lo, hi, blocks))
    return out


def phase_ffn(kb, dr, l, tiles, hout_fn, lng, lnb):
    nc = kb.nc
    wup = dr["ffn_w_up"][l].rearrange("(kc p) n -> p kc n", p=128)
    wdn = dr["ffn_w_down"][l].rearrange("(fc p) n -> p fc n", p=128)
    with contextlib.ExitStack() as es:
        kb.start_phase()
        cs = make_consts(kb, es)
        lr = LNRes(kb, es)
        psa = Pool(kb, es, "psa", [128, 512], F32, 2, psum=True)
        psv = Pool(kb, es, "psv", [128, 512], F32, 2, psum=True)
        psd = Pool(kb, es, "psd", [128, 512], F32, 2, psum=True)
        pstr = Pool(kb, es, "pstr", [128, 512], F32, 2, psum=True)
        mods = load_modT(kb, es, cs, dr, l, "modT", pstr)
        G, Gb = load_bcast(kb, es, "lnG", lng, D)
        Bt, Bb = load_bcast(kb, es, "lnB", lnb, D)
        cws = [load_colT(kb, es, cs, f"convw{tap}", dr["ffn_conv_w"][l, tap], FC, pstr) for tap in range(3)]
        cbias, cbb = load_colT(kb, es, cs, "convb", dr["ffn_conv_b"][l], FC, pstr)
        hT, hTb = sb(kb, es, "hT", [128, FC, 768], BF16)
        for (g, lo, hi, blocks) in ffn_groups(tiles):
            ntok = len(g) * 128
            ncols = hi - lo
            with contextlib.ExitStack() as es2:
                ug, ugb = sb(kb, es2, "ug", [128, KC, 772], BF16)
                kb.dma(kb.sp, out=ug[:, :, 0:ncols], in_=dr["uf"][:, :, lo:hi], sbuf_buf=ugb, begins=[ugb])
                for pc in (0, 257, 258, 2307):
                    if lo <= pc < hi:
                        kb.op(kb.dve, nc.vector.memset, ug[:, :, pc - lo:pc - lo + 1], 0.0, reads=[ugb], writes=[ugb])
                wup_p = Pool(kb, es2, "wu", [128, KC, 2, 128], BF16, 3)
                c1p = Pool(kb, es2, "c1", [128, 386], F32, 3)
                c2p = Pool(kb, es2, "c2", [128, 386], F32, 3)
                for fc in range(FC):
                    wu, wub = wup_p.next()
                    kb.dma(kb.pool, out=wu[:, :, 0, :], in_=wup[:, :, fc * 128:(fc + 1) * 128], sbuf_buf=wub, begins=[wub])
                    kb.dma(kb.pool, out=wu[:, :, 1, :], in_=wup[:, :, DFF + fc * 128:DFF + (fc + 1) * 128], sbuf_buf=wub,
                           writes=[wub])
                    for (c0, n, t0) in blocks:
                        cc = c0 - lo
                        pa, pab = psa.next()
                        for kc in range(KC):
                            kb.op(kb.pe, nc.tensor.matmul, pa[:, 0:n], lhsT=wu[:, kc, 0, :], rhs=ug[:, kc, cc:cc + n],
                                  start=(kc == 0), stop=(kc == KC - 1), reads=[wub, ugb], begins=[pab] if kc == 0 else (),
                                  writes=[pab] if kc else (), signal=(kc == KC - 1))
                        pv, pvb = psv.next()
                        for kc in range(KC):
                            kb.op(kb.pe, nc.tensor.matmul, pv[:, 0:n - 2], lhsT=wu[:, kc, 1, :], rhs=ug[:, kc, cc + 1:cc + n - 1],
                                  start=(kc == 0), stop=(kc == KC - 1), reads=[wub, ugb], begins=[pvb] if kc == 0 else (),
                                  writes=[pvb] if kc else (), signal=(kc == KC - 1))
                        m = n - 2
                        c1, c1b = c1p.next()
                        kb.op(kb.act, nc.scalar.activation, out=c1[:, 0:m], in_=pa[:, 1:n - 1], func=AF.Identity,
                              scale=cws[1][0][:, fc:fc + 1], bias=cbias[:, fc:fc + 1], reads=[pab, cws[1][1], cbb], begins=[c1b])
                        c2, c2b = c2p.next()
                        kb.op(kb.dve, nc.vector.scalar_tensor_tensor, out=c2[:, 0:m], in0=pa[:, 0:m], scalar=cws[0][0][:, fc:fc + 1],
                              in1=c1[:, 0:m], op0=ALU.mult, op1=ALU.add, reads=[pab, cws[0][1], c1b], begins=[c2b])
                        kb.op(kb.dve, nc.vector.scalar_tensor_tensor, out=c1[:, 0:m], in0=pa[:, 2:n], scalar=cws[2][0][:, fc:fc + 1],
                              in1=c2[:, 0:m], op0=ALU.mult, op1=ALU.add, reads=[pab, cws[2][1], c2b], writes=[c1b])
                        kb.op(kb.act, nc.scalar.activation, out=c2[:, 0:m], in_=c1[:, 0:m], func=AF.Gelu, reads=[c1b], writes=[c2b])
                        kb.op(kb.dve, nc.vector.tensor_tensor, out=hT[:, fc, t0:t0 + m], in0=c2[:, 0:m], in1=pv[:, 0:m], op=ALU.mult,
                              reads=[c2b, pvb], begins=[hTb] if (fc == 0 and t0 == 0) else (),
                              writes=() if (fc == 0 and t0 == 0) else [hTb])
                kb.barrier()
            with contextlib.ExitStack() as es2:
                wdp = Pool(kb, es2, "wd", [128, FC, 128], BF16, 3)
                zs = [sb(kb, es2, f"z{i}", [128, D], F32) for i in range(len(g))]
                fgp = Pool(kb, es2, "fg", [128, 384], F32, 3)
                for i, tt in enumerate(g):
                    kb.dma(kb.sp, out=zs[i][0][:], in_=dr["hmid"][tt * 128:(tt + 1) * 128, :], sbuf_buf=zs[i][1], begins=[zs[i][1]])
                r_of = [0 if tt >= 2 else 1 for tt in g]
                nblk = [(b0, min(384, ntok - b0)) for b0 in range(0, ntok, 384)]
                for oc in range(KC):
                    wd, wdb = wdp.next()
                    kb.dma(kb.pool, out=wd[:], in_=wdn[:, :, oc * 128:(oc + 1) * 128], sbuf_buf=wdb, begins=[wdb])
                    for (b0, bn) in nblk:
                        pd, pdb = psd.next()
                        for fc in range(FC):
                            kb.op(kb.pe, nc.tensor.matmul, pd[:, 0:bn], lhsT=wd[:, fc, :], rhs=hT[:, fc, b0:b0 + bn],
                                  start=(fc == 0), stop=(fc == FC - 1), reads=[wdb, hTb], begins=[pdb] if fc == 0 else (),
                                  writes=[pdb] if fc else (), signal=(fc == FC - 1))
                        fg, fgb = fgp.next()
                        first = True
                        for j in range(bn // 128):
                            i = (b0 // 128) + j
                            kb.op(kb.act, nc.scalar.activation, out=fg[:, j * 128:(j + 1) * 128], in_=pd[:, j * 128:(j + 1) * 128],
                                  func=AF.Identity, scale=mods[r_of[i]][0][:, 80 + oc:81 + oc], reads=[pdb, mods[r_of[i]][1]],
                                  begins=[fgb] if first else (), writes=() if first else [fgb])
                            first = False
                        ptr, ptrb = pstr.next()
                        for j in range(bn // 128):
                            kb.op(kb.pe, nc.tensor.transpose, out=ptr[:, j * 128:(j + 1) * 128], in_=fg[:, j * 128:(j + 1) * 128],
                                  identity=cs.identf[:], reads=[fgb, cs.identf_b], begins=[ptrb] if j == 0 else (),
                                  writes=[ptrb] if j else (), signal=(j == bn // 128 - 1))
                        for j in range(bn // 128):
                            i = (b0 // 128) + j
                            z, zb = zs[i]
                            kb.op(kb.dve, nc.vector.scalar_tensor_tensor, out=z[:, oc * 128:(oc + 1) * 128],
                                  in0=z[:, oc * 128:(oc + 1) * 128], scalar=ALPHA, in1=ptr[:, j * 128:(j + 1) * 128],
                                  op0=ALU.mult, op1=ALU.add, reads=[ptrb, zb], writes=[zb])
                for i, tt in enumerate(g):
                    z, zb = zs[i]
                    rs, rsb, nm, nmb = emit_ln_stats(kb, cs, lr, z, zb)
                    kb.op(kb.act, nc.scalar.activation, out=z[:], in_=z[:], func=AF.Identity, scale=rs[:, 0:1], bias=nm[:, 0:1],
                          reads=[zb, rsb, nmb], writes=[zb])
                    kb.op(kb.pool, nc.gpsimd.tensor_tensor, out=z[:], in0=z[:], in1=G[:], op=ALU.mult, reads=[zb, Gb], writes=[zb])
                    kb.op(kb.pool, nc.gpsimd.tensor_tensor, out=z[:], in0=z[:], in1=Bt[:], op=ALU.add, reads=[zb, Bb], writes=[zb])
                    kb.dma(kb.sp, out=hout_fn(tt), in_=z[:], sbuf_buf=zb, reads=[zb])
                kb.barrier()
        kb.end_phase()


LAM_INIT0 = 0.8 - 0.6 * math.exp(-0.3 * 0)


def phase_even(kb, dr, heads=range(12), do_fourier=True):
    nc = kb.nc
    win = dr["even_w_in"].rearrange("(kc p) n -> p kc n", p=128)
    cat = dr["cat0"]
    with contextlib.ExitStack() as es:
        kb.start_phase()
        cs = make_consts(kb, es)
        lr = LNRes(kb, es)
        psA = Pool(kb, es, "psA", [128, 512], F32, 2, psum=True)
        psS = Pool(kb, es, "psS", [128, 512], F32, 2, psum=True)
        psO = Pool(kb, es, "psO", [128, 512], F32, 1, psum=True)
        psL = Pool(kb, es, "psL", [128, 512], F32, 1, psum=True)
        pst = Pool(kb, es, "pst", [128, 1024], BF16, 2, psum=True)
        mods = load_modT(kb, es, cs, dr, 0, "modT", psA)
        uT, uTb = sb(kb, es, "uT", [128, KC, T], BF16)
        with contextlib.ExitStack() as es2:
            xp = Pool(kb, es2, "xin", [128, D], F32, 2)
            nbp = Pool(kb, es2, "nb", [128, D], BF16, 2)
            for tt in range(NT):
                r = 0 if tt >= 2 else 1
                x, xb = xp.next()
                src = dr["ctx"][tt * 128:(tt + 1) * 128, :] if tt < 2 else dr["x"][(tt - 2) * 128:(tt - 1) * 128, :]
                kb.dma(kb.sp, out=x[:], in_=src, sbuf_buf=xb, begins=[xb])
                emit_ln_to_uT(kb, cs, lr, x, xb, nbp, pst, mods[r][0][:, 16:32], mods[r][0][:, 0:16], mods[r][1], uT, uTb,
                              tt * 128, tt == 0)
            kb.barrier()
        lv, lvb = load_bcast(kb, es, "lamv", dr["diff_lambda"].rearrange("a b -> (a b)"), 256)
        lam, lamb = sb(kb, es, "lam", [128, 4], F32)
        lt, ltb = sb(kb, es, "lamt", [128, 2, 64], F32)
        kb.op(kb.dve, nc.vector.tensor_tensor, out=lt[:, 0, :], in0=lv[:, 0:64], in1=lv[:, 64:128], op=ALU.mult, reads=[lvb], begins=[ltb])
        kb.op(kb.dve, nc.vector.tensor_tensor, out=lt[:, 1, :], in0=lv[:, 128:192], in1=lv[:, 192:256], op=ALU.mult, reads=[lvb], writes=[ltb])
        kb.op(kb.dve, nc.vector.reduce_sum, out=lam[:, 0:2], in_=lt[:], axis=AX.X, reads=[ltb], begins=[lamb])
        kb.op(kb.act, nc.scalar.activation, out=lam[:, 0:2], in_=lam[:, 0:2], func=AF.Exp, reads=[lamb], writes=[lamb])
        kb.op(kb.dve, nc.vector.tensor_tensor, out=lam[:, 2:3], in0=lam[:, 1:2], in1=lam[:, 0:1], op=ALU.subtract, reads=[lamb], writes=[lamb])
        kb.op(kb.dve, nc.vector.tensor_scalar, out=lam[:, 3:4], in0=lam[:, 2:3], scalar1=-LAM_INIT0, scalar2=None, op0=ALU.add,
              reads=[lamb], writes=[lamb])
        sw, swb = sb(kb, es, "subln", [128, 1], F32)
        kb.dma(kb.sp, out=sw[:], in_=dr["diff_subln"].rearrange("(p o) -> p o", o=1), sbuf_buf=swb, begins=[swb])
        kb.op(kb.dve, nc.vector.tensor_scalar, out=sw[:], in0=sw[:], scalar1=(1.0 - LAM_INIT0), scalar2=None, op0=ALU.mult,
              reads=[swb], writes=[swb])
        if len(list(heads)) > 0:
          with contextlib.ExitStack() as es2:
            COS, COSb = sb(kb, es2, "cos", [128, NLAT], F32)
            SIN, SINb = sb(kb, es2, "sin", [128, NLAT], F32)
            kb.dma(kb.sp, out=COS[:], in_=dr["rope_cos"], sbuf_buf=COSb, begins=[COSb])
            kb.dma(kb.sp, out=SIN[:], in_=dr["rope_sin"], sbuf_buf=SINb, begins=[SINb])
            wp = Pool(kb, es2, "wqkv", [128, KC, 3, 128], BF16, 2)
            wsp = Pool(kb, es2, "wsw", [128, KC, 2, 128], BF16, 2)
            qTp = Pool(kb, es2, "qT", [128, T], BF16, 2)
            kTp = Pool(kb, es2, "kT", [128, T], BF16, 2)
            vp = Pool(kb, es2, "vtok", [128, NT, 128], BF16, 2)
            t1p = Pool(kb, es2, "rt1", [128, 512], F32, 2)
            t2p = Pool(kb, es2, "rt2", [128, 512], F32, 2)
            pp = Pool(kb, es2, "pexp", [128, 512], BF16, 3)
            rlp = Pool(kb, es2, "rl", [128, 512], F32, 2)
            op_ = Pool(kb, es2, "om", [128, 2, 512], F32, 2)
            sqp = Pool(kb, es2, "sq", [128, 512], F32, 2)
            yp = Pool(kb, es2, "yat", [128, 512], BF16, 2)
            for h in heads:
                w, wb = wp.next()
                for i, base in enumerate((0, 1536, 3072)):
                    kb.dma(kb.pool, out=w[:, :, i, :], in_=win[:, :, base + h * 128:base + (h + 1) * 128], sbuf_buf=wb,
                           begins=[wb] if i == 0 else (), writes=[wb] if i else ())
                ws, wsb = wsp.next()
                for i in range(2):
                    src = w[:, :, i, :].rearrange("p k (g hh e) -> p k g hh e", g=4, hh=2, e=16)
                    dst = ws[:, :, i, :].rearrange("p k (g hh e) -> p k g hh e", g=4, hh=2, e=16)
                    for hh in range(2):
                        kb.op(kb.dve, nc.vector.tensor_copy, out=dst[:, :, :, hh, :], in_=src[:, :, :, 1 - hh, :], reads=[wb],
                              begins=[wsb] if (i == 0 and hh == 0) else (), writes=() if (i == 0 and hh == 0) else [wsb])
                qT, qTb = qTp.next()
                kT, kTb = kTp.next()
                first = {0: True, 1: True}
                for i, (dst, dstb) in enumerate(((qT, qTb), (kT, kTb))):
                    ps, psb = psA.next()
                    for kc in range(KC):
                        kb.op(kb.pe, nc.tensor.matmul, ps[:, 0:NCTX], lhsT=w[:, kc, i, :], rhs=uT[:, kc, 0:NCTX], start=(kc == 0),
                              stop=(kc == KC - 1), reads=[wb, uTb], begins=[psb] if kc == 0 else (), writes=[psb] if kc else (),
                              signal=(kc == KC - 1))
                    kb.op(kb.act, nc.scalar.activation, out=dst[:, 0:NCTX], in_=ps[:, 0:NCTX], func=AF.Identity, scale=1.0,
                          reads=[psb], begins=[dstb])
                    for qb in range(4):
                        c0 = NCTX + qb * 512
                        ps, psb = psA.next()
                        for kc in range(KC):
                            kb.op(kb.pe, nc.tensor.matmul, ps[:], lhsT=w[:, kc, i, :], rhs=uT[:, kc, c0:c0 + 512], start=(kc == 0),
                                  stop=(kc == KC - 1), reads=[wb, uTb], begins=[psb] if kc == 0 else (), writes=[psb] if kc else (),
                                  signal=(kc == KC - 1))
                        t1, t1b = t1p.next()
                        kb.op(kb.dve, nc.vector.tensor_tensor, out=t1[:], in0=ps[:], in1=COS[:, qb * 512:(qb + 1) * 512], op=ALU.mult,
                              reads=[psb, COSb], begins=[t1b])
                        ps2, ps2b = psA.next()
                        for kc in range(KC):
                            kb.op(kb.pe, nc.tensor.matmul, ps2[:], lhsT=ws[:, kc, i, :], rhs=uT[:, kc, c0:c0 + 512], start=(kc == 0),
                                  stop=(kc == KC - 1), reads=[wsb, uTb], begins=[ps2b] if kc == 0 else (), writes=[ps2b] if kc else (),
                                  signal=(kc == KC - 1))
                        t2, t2b = t2p.next()
                        kb.op(kb.dve, nc.vector.tensor_tensor, out=t2[:], in0=ps2[:], in1=SIN[:, qb * 512:(qb + 1) * 512], op=ALU.mult,
                              reads=[ps2b, SINb], begins=[t2b])
                        kb.op(kb.pool, nc.gpsimd.tensor_tensor, out=dst[:, c0:c0 + 512], in0=t1[:], in1=t2[:], op=ALU.add,
                              reads=[t1b, t2b], writes=[dstb])
                v, vb = vp.next()
                for tt in range(NT):
                    ps, psb = psA.next()
                    for kc in range(KC):
                        kb.op(kb.pe, nc.tensor.matmul, ps[:, 0:128], lhsT=uT[:, kc, tt * 128:(tt + 1) * 128], rhs=w[:, kc, 2, :],
                              start=(kc == 0), stop=(kc == KC - 1), reads=[wb, uTb], begins=[psb] if kc == 0 else (),
                              writes=[psb] if kc else (), signal=(kc == KC - 1))
                    kb.op(kb.act, nc.scalar.activation, out=v[:, tt, :], in_=ps[:, 0:128], func=AF.Identity, scale=1.0, reads=[psb],
                          begins=[vb] if tt == 0 else (), writes=[vb] if tt else ())
                blocks = [(0, NCTX, [0, 1], 0)] + [(NCTX + qb * 512, 512, list(range(NT)), NCTX + qb * 512) for qb in range(4)]
                for (q0, nq, kts, cc0) in blocks:
                    om, omb = op_.next()
                    for m in range(2):
                        r0 = m * 64
                        po, pob = psO.next()
                        pl, plb = psL.next()
                        def score(kt):
                            ps, psb = psS.next()
                            kb.op(kb.pe, nc.tensor.matmul, ps[:, 0:nq], lhsT=kT[r0:r0 + 64, kt * 128:(kt + 1) * 128],
                                  rhs=qT[r0:r0 + 64, q0:q0 + nq], start=True, stop=True, reads=[kTb, qTb], begins=[psb])
                            return ps, psb
                        nxt = score(kts[0])
                        for ki, kt in enumerate(kts):
                            ps, psb = nxt
                            if ki + 1 < len(kts):
                                nxt = score(kts[ki + 1])
                            p, pb = pp.next()
                            kb.op(kb.act, nc.scalar.activation, out=p[:, 0:nq], in_=ps[:, 0:nq], func=AF.Exp, scale=0.125, reads=[psb],
                                  begins=[pb])
                            kb.op(kb.pe, nc.tensor.matmul, po[:, 0:nq], lhsT=v[:, kt, :], rhs=p[:, 0:nq], start=(ki == 0),
                                  stop=(ki == len(kts) - 1), reads=[vb, pb], begins=[pob] if ki == 0 else (), writes=[pob] if ki else (),
                                  signal=False)
                            kb.op(kb.pe, nc.tensor.matmul, pl[:, 0:nq], lhsT=cs.onesb[:], rhs=p[:, 0:nq], start=(ki == 0),
                                  stop=(ki == len(kts) - 1), reads=[cs.onesb_b, pb], begins=[plb] if ki == 0 else (),
                                  writes=[plb] if ki else (), signal=True)
                        rl, rlb = rlp.next()
                        kb.op(kb.dve, nc.vector.reciprocal, out=rl[:, 0:nq], in_=pl[:, 0:nq], reads=[plb], begins=[rlb])
                        kb.op(kb.dve, nc.vector.tensor_tensor, out=om[:, m, 0:nq], in0=po[:, 0:nq], in1=rl[:, 0:nq], op=ALU.mult,
                              reads=[pob, rlb], begins=[omb] if m == 0 else (), writes=[omb] if m else ())
                    kb.op(kb.dve, nc.vector.scalar_tensor_tensor, out=om[:, 0, 0:nq], in0=om[:, 1, 0:nq], scalar=lam[:, 3:4],
                          in1=om[:, 0, 0:nq], op0=ALU.mult, op1=ALU.add, reads=[omb, lamb], writes=[omb])
                    sq, sqb = sqp.next()
                    kb.op(kb.act, nc.scalar.activation, out=sq[:, 0:nq], in_=om[:, 0, 0:nq], func=AF.Square, reads=[omb], begins=[sqb])
                    pm, pmb = psS.next()
                    kb.op(kb.pe, nc.tensor.matmul, pm[:, 0:nq], lhsT=cs.onesf[:], rhs=sq[:, 0:nq], start=True, stop=True,
                          reads=[cs.onesf_b, sqb], begins=[pmb])
                    kb.op(kb.act, nc.scalar.activation, out=sq[:, 0:nq], in_=pm[:, 0:nq], func=AF.Sqrt, scale=1.0 / 128, bias=cs.epsr[:],
                          reads=[pmb, cs.epsr_b], writes=[sqb])
                    kb.op(kb.dve, nc.vector.reciprocal, out=sq[:, 0:nq], in_=sq[:, 0:nq], reads=[sqb], writes=[sqb])
                    kb.op(kb.dve, nc.vector.tensor_tensor, out=sq[:, 0:nq], in0=sq[:, 0:nq], in1=om[:, 0, 0:nq], op=ALU.mult,
                          reads=[sqb, omb], writes=[sqb])
                    y, yb = yp.next()
                    kb.op(kb.act, nc.scalar.activation, out=y[:, 0:nq], in_=sq[:, 0:nq], func=AF.Identity, scale=sw[:, 0:1],
                          reads=[sqb, swb], begins=[yb])
                    kb.dma(kb.sp, out=cat[h * 128:(h + 1) * 128, cc0:cc0 + nq], in_=y[:, 0:nq], sbuf_buf=yb, reads=[yb])
            kb.barrier()
        if do_fourier:
          with contextlib.ExitStack() as es2:
            cs128, cs128b = sb(kb, es2, "cs128", [128, 256], BF16)
            kb.dma(kb.sp, out=cs128[:], in_=dr["dft128"], sbuf_buf=cs128b, begins=[cs128b])
            d256, d256b = sb(kb, es2, "d256", [128, 2, 2, 256], BF16)
            for tb_ in range(2):
                kb.dma(kb.sp, out=d256[:, tb_, :, :], in_=dr["dft256"][tb_].rearrange("(tt p) n -> p tt n", p=128), sbuf_buf=d256b,
                       begins=[d256b] if tb_ == 0 else (), writes=[d256b] if tb_ else ())
            AB, ABb = sb(kb, es2, "AB", [128, 4, NT, 256], BF16)
            wfp = Pool(kb, es2, "wf", [128, KC, 128], BF16, 2)
            fTp = Pool(kb, es2, "fT", [128, T], BF16, 2)
            for g in range(4):
                wf, wfb = wfp.next()
                kb.dma(kb.pool, out=wf[:], in_=win[:, :, 4608 + g * 128:4608 + (g + 1) * 128], sbuf_buf=wfb, begins=[wfb])
                fT, fTb = fTp.next()
                for bi, (c0, n) in enumerate([(0, 256)] + [(NCTX + i * 512, 512) for i in range(4)]):
                    ps, psb = psA.next()
                    for kc in range(KC):
                        kb.op(kb.pe, nc.tensor.matmul, ps[:, 0:n], lhsT=wf[:, kc, :], rhs=uT[:, kc, c0:c0 + n], start=(kc == 0),
                              stop=(kc == KC - 1), reads=[wfb, uTb], begins=[psb] if kc == 0 else (), writes=[psb] if kc else (),
                              signal=(kc == KC - 1))
                    kb.op(kb.act, nc.scalar.activation, out=fT[:, c0:c0 + n], in_=ps[:, 0:n], func=AF.Identity, scale=1.0, reads=[psb],
                          begins=[fTb] if bi == 0 else (), writes=[fTb] if bi else ())
                for tt in range(NT):
                    ps, psb = psS.next()
                    kb.op(kb.pe, nc.tensor.matmul, ps[:, 0:256], lhsT=fT[:, tt * 128:(tt + 1) * 128], rhs=cs128[:], start=True, stop=True,
                          reads=[fTb, cs128b], begins=[psb])
                    first = (g == 0 and tt == 0)
                    kb.op(kb.dve, nc.vector.tensor_copy, out=AB[:, g, tt, :], in_=ps[:, 0:256], reads=[psb],
                          begins=[ABb] if first else (), writes=() if first else [ABb])
            yfp = Pool(kb, es2, "yf", [128, 512], BF16, 2)
            for g in range(4):
                ps, psb = psA.next()
                k = 0
                for tt in range(2):
                    for tb_ in range(2):
                        kb.op(kb.pe, nc.tensor.matmul, ps[:, 0:256], lhsT=AB[:, g, tt, tb_ * 128:(tb_ + 1) * 128], rhs=d256[:, tb_, tt, :],
                              start=(k == 0), stop=(k == 3), reads=[ABb, d256b], begins=[psb] if k == 0 else (), writes=[psb] if k else (),
                              signal=(k == 3))
                        k += 1
                yf, yfb = yfp.next()
                kb.op(kb.act, nc.scalar.activation, out=yf[:, 0:256], in_=ps[:, 0:256], func=AF.Identity, scale=(256 * 128) ** -0.5,
                      reads=[psb], begins=[yfb])
                kb.dma(kb.sp, out=cat[1536 + g * 128:1536 + (g + 1) * 128, 0:NCTX], in_=yf[:, 0:256], sbuf_buf=yfb, reads=[yfb])
            slp = Pool(kb, es2, "dslab", [128, 2, 16, 512], BF16, 2)
            for pb_ in range(4):
                sl, slb = slp.next()
                for tb_ in range(2):
                    kb.dma(kb.sp, out=sl[:, tb_, :, :], in_=dr["dft2048"][tb_, :, pb_ * 512:(pb_ + 1) * 512].rearrange("(tt p) n -> p tt n", p=128),
                           sbuf_buf=slb, begins=[slb] if tb_ == 0 else (), writes=[slb] if tb_ else ())
                for g in range(4):
                    ps, psb = psA.next()
                    k = 0
                    for tt in range(16):
                        for tb_ in range(2):
                            kb.op(kb.pe, nc.tensor.matmul, ps[:], lhsT=AB[:, g, 2 + tt, tb_ * 128:(tb_ + 1) * 128], rhs=sl[:, tb_, tt, :],
                                  start=(k == 0), stop=(k == 31), reads=[ABb, slb], begins=[psb] if k == 0 else (),
                                  writes=[psb] if k else (), signal=(k == 31))
                            k += 1
                    yf, yfb = yfp.next()
                    kb.op(kb.act, nc.scalar.activation, out=yf[:], in_=ps[:], func=AF.Identity, scale=(2048 * 128) ** -0.5, reads=[psb],
                          begins=[yfb])
                    kb.dma(kb.sp, out=cat[1536 + g * 128:1536 + (g + 1) * 128, NCTX + pb_ * 512:NCTX + (pb_ + 1) * 512], in_=yf[:],
                           sbuf_buf=yfb, reads=[yfb])
            kb.barrier()
        kb.end_phase()


def host_tables():
    inv = 1.0 / (10000.0 ** (np.arange(0, 32, 2, dtype=np.float32) / 32))
    tok = np.arange(NLAT)
    rows, cols = tok // 64, tok % 64
    ang_r = rows[None, :].astype(np.float32) * inv[:, None]
    ang_c = cols[None, :].astype(np.float32) * inv[:, None]
    cos64 = np.concatenate([np.cos(ang_r), np.cos(ang_r), np.cos(ang_c), np.cos(ang_c)], 0)
    sin64 = np.concatenate([-np.sin(ang_r), np.sin(ang_r), -np.sin(ang_c), np.sin(ang_c)], 0)
    cos = np.concatenate([cos64, cos64], 0).astype(np.float32)
    sin = np.concatenate([sin64, sin64], 0).astype(np.float32)

    def dft(n):
        i = np.arange(n, dtype=np.int64)
        a = 2.0 * np.pi * ((i[:, None] * i[None, :]) % n).astype(np.float64) / n
        return np.cos(a), np.sin(a)
    c128, s128 = dft(128)
    d128 = np.concatenate([c128, -s128], 1).astype(ml_dtypes.bfloat16)
    c256, s256 = dft(256)
    d256 = np.stack([c256, s256]).astype(ml_dtypes.bfloat16)
    c2k, s2k = dft(2048)
    d2k = np.stack([c2k, s2k]).astype(ml_dtypes.bfloat16)
    return {"rope_cos": cos, "rope_sin": sin, "dft128": d128, "dft256": d256, "dft2048": d2k}


NCH = T // 64


def phase_hgrn(kb, dr, heads=range(16)):
    nc = kb.nc
    win = dr["hgrn_w_in"].rearrange("(kc p) n -> p kc n", p=128)
    cat = dr["cat1"]
    blocks5 = [(0, 256)] + [(NCTX + i * 512, 512) for i in range(4)]
    with contextlib.ExitStack() as es:
        kb.start_phase()
        cs = make_consts(kb, es)
        lr = LNRes(kb, es)
        psA = Pool(kb, es, "psA", [128, 512], F32, 2, psum=True)
        psS = Pool(kb, es, "psS", [128, 512], F32, 2, psum=True)
        psO = Pool(kb, es, "psO", [128, 512], F32, 1, psum=True)
        psK = Pool(kb, es, "psK", [128, 512], F32, 1, psum=True)
        pst = Pool(kb, es, "pst", [128, 1024], BF16, 2, psum=True)
        mods = load_modT(kb, es, cs, dr, 1, "modT", psA)
        uT, uTb = sb(kb, es, "uT", [128, KC, T], BF16)
        with contextlib.ExitStack() as es2:
            xp = Pool(kb, es2, "xin", [128, D], F32, 2)
            nbp = Pool(kb, es2, "nb", [128, D], BF16, 2)
            for tt in range(NT):
                r = 0 if tt >= 2 else 1
                x, xb = xp.next()
                kb.dma(kb.sp, out=x[:], in_=dr["hres"][tt * 128:(tt + 1) * 128, :], sbuf_buf=xb, begins=[xb])
                emit_ln_to_uT(kb, cs, lr, x, xb, nbp, pst, mods[r][0][:, 16:32], mods[r][0][:, 0:16], mods[r][1], uT, uTb,
                              tt * 128, tt == 0)
            kb.barrier()
        lbs = []
        for d in range(2):
            b0, b0b = load_colT(kb, es, cs, f"lb0{d}", dr["hgrn_lower_bounds"][d, 0], 16, psA)
            b1, b1b = load_colT(kb, es, cs, f"lb1{d}", dr["hgrn_lower_bounds"][d, 1], 16, psA)
            kb.op(kb.dve, nc.vector.tensor_tensor, out=b1[:], in0=b1[:], in1=b0[:], op=ALU.subtract, reads=[b0b, b1b], writes=[b1b])
            kb.op(kb.act, nc.scalar.activation, out=b1[:], in_=b1[:], func=AF.Sigmoid, reads=[b1b], writes=[b1b])
            kb.op(kb.dve, nc.vector.tensor_scalar, out=b0[:], in0=b1[:], scalar1=-1.0, scalar2=1.0, op0=ALU.mult, op1=ALU.add,
                  reads=[b1b], writes=[b0b])
            nml, nmlb = sb(kb, es, f"noml{d}", [128, 16], F32)
            kb.op(kb.dve, nc.vector.tensor_scalar, out=nml[:], in0=b0[:], scalar1=-1.0, scalar2=None, op0=ALU.mult, reads=[b0b], begins=[nmlb])
            lbs.append((b1, b1b, b0, b0b, nml, nmlb))
        NW, NWb = load_bcast(kb, es, "hnw", dr["hgrn_norm"], 128)
        m01, m01b = sb(kb, es, "m01", [128, T], F32)
        kb.op(kb.pool, nc.gpsimd.memset, m01[:], 1.0, begins=[m01b])
        kb.op(kb.pool, nc.gpsimd.memset, m01[:].rearrange("p (c s) -> p c s", s=64)[:, :, 0:1], 0.0, reads=[m01b], writes=[m01b])
        masks = []
        for d in range(2):
            mk, mkb = sb(kb, es, f"tri{d}", [128, 64], F32)
            kb.op(kb.pool, nc.gpsimd.memset, mk[:], 1.0, begins=[mkb])
            for r0 in (0, 64):
                kb.op(kb.pool, nc.gpsimd.affine_select, out=mk[r0:r0 + 64, :], in_=mk[r0:r0 + 64, :],
                      pattern=[[1 if d == 0 else -1, 64]], compare_op=ALU.is_ge, fill=0.0, base=0,
                      channel_multiplier=(-1 if d == 0 else 1), reads=[mkb], writes=[mkb])
            masks.append((mk, mkb))
        w5, w5b = sb(kb, es, "w5", [128, KC, 5, 128], BF16)
        qsil, qsilb = sb(kb, es, "qsil", [128, T], F32)
        A, Ab = sb(kb, es, "hA", [128, T], F32)
        Bk, Bkb = sb(kb, es, "hB", [128, T], F32)
        C, Cb = sb(kb, es, "hC", [128, T], F32)
        qt, qtb = sb(kb, es, "hqt", [128, T], BF16)
        kt, ktb = sb(kb, es, "hkt", [128, T], BF16)
        kh, khb = sb(kb, es, "hkh", [128, T], BF16)
        ktok, ktokb = sb(kb, es, "hktok", [128, NT, 128], BF16)
        v, vb = sb(kb, es, "hv", [128, NT, 128], BF16)
        gsil, gsilb = sb(kb, es, "hg", [128, 16, 128], F32)
        oacc, oaccb = sb(kb, es, "hoacc", [128, 16, 128], F32)
        ref, refb = sb(kb, es, "href", [128, NCH], F32)
        cend, cendb = sb(kb, es, "hcend", [128, NCH], F32)
        gam, gamb = sb(kb, es, "hgam", [128, NCH], F32)
        tot, totb = sb(kb, es, "htot", [128, NCH], F32)
        Sm, Smb_ = sb(kb, es, "hSm", [128, 128], F32)
        Sbp = Pool(kb, es, "hSb", [128, 128], BF16, 2)
        scp = Pool(kb, es, "hsc", [128, 64], BF16, 2)
        ssq, ssqb = sb(kb, es, "hssq", [128, 16], F32)
        junk, junkb = sb(kb, es, "hjunk", [128, 128], F32)
        ytp = Pool(kb, es, "hyt", [128, 128], BF16, 2)
        yTp = Pool(kb, es, "hyT", [128, 512], BF16, 2)
        secs = (0, 2048, 4096, 6144, 8192)
        v3 = lambda t: t[:].rearrange("p (c s) -> p c s", s=64)
        for h in heads:
            for i, base in enumerate(secs):
                kb.dma(kb.pool, out=w5[:, :, i, :], in_=win[:, :, base + h * 128:base + (h + 1) * 128], sbuf_buf=w5b,
                       begins=[w5b] if i == 0 else (), writes=[w5b] if i else ())
            for tt in range(NT):
                ps, psb = psA.next()
                for kc in range(KC):
                    kb.op(kb.pe, nc.tensor.matmul, ps[:, 0:256], lhsT=uT[:, kc, tt * 128:(tt + 1) * 128], rhs=w5[:, kc, 3:5, :],
                          start=(kc == 0), stop=(kc == KC - 1), reads=[w5b, uTb], begins=[psb] if kc == 0 else (),
                          writes=[psb] if kc else (), signal=(kc == KC - 1))
                kb.op(kb.act, nc.scalar.activation, out=v[:, tt, :], in_=ps[:, 0:128], func=AF.Identity, scale=1.0, reads=[psb],
                      begins=[vb] if tt == 0 else (), writes=[vb] if tt else ())
                if tt >= 2:
                    kb.op(kb.act, nc.scalar.activation, out=gsil[:, tt - 2, :], in_=ps[:, 128:256], func=AF.Silu, reads=[psb],
                          begins=[gsilb] if tt == 2 else (), writes=[gsilb] if tt > 2 else ())
            for bi, (c0, n) in enumerate(blocks5):
                ps, psb = psA.next()
                for kc in range(KC):
                    kb.op(kb.pe, nc.tensor.matmul, ps[:, 0:n], lhsT=w5[:, kc, 0, :], rhs=uT[:, kc, c0:c0 + n], start=(kc == 0),
                          stop=(kc == KC - 1), reads=[w5b, uTb], begins=[psb] if kc == 0 else (), writes=[psb] if kc else (),
                          signal=(kc == KC - 1))
                kb.op(kb.act, nc.scalar.activation, out=qsil[:, c0:c0 + n], in_=ps[:, 0:n], func=AF.Silu, reads=[psb],
                      begins=[qsilb] if bi == 0 else (), writes=[qsilb] if bi else ())
            for d in range(2):
                lb, lbb, oml, omlb, nml, nmlb = lbs[d]
                mk, mkb = masks[d]
                for bi, (c0, n) in enumerate(blocks5):
                    ps, psb = psA.next()
                    for kc in range(KC):
                        kb.op(kb.pe, nc.tensor.matmul, ps[:, 0:n], lhsT=w5[:, kc, 1 + d, :], rhs=uT[:, kc, c0:c0 + n], start=(kc == 0),
                              stop=(kc == KC - 1), reads=[w5b, uTb], begins=[psb] if kc == 0 else (), writes=[psb] if kc else (),
                              signal=(kc == KC - 1))
                    kb.op(kb.act, nc.scalar.activation, out=A[:, c0:c0 + n], in_=ps[:, 0:n], func=AF.Sigmoid, reads=[psb],
                          begins=[Ab] if bi == 0 else (), writes=[Ab] if bi else ())
                kb.op(kb.dve, nc.vector.tensor_scalar, out=Bk[:], in0=A[:], scalar1=nml[:, h:h + 1], scalar2=oml[:, h:h + 1],
                      op0=ALU.mult, op1=ALU.add, reads=[Ab, nmlb, omlb], begins=[Bkb])
                kb.op(kb.dve, nc.vector.tensor_scalar, out=A[:], in0=A[:], scalar1=oml[:, h:h + 1], scalar2=lb[:, h:h + 1],
                      op0=ALU.mult, op1=ALU.add, reads=[Ab, omlb, lbb], writes=[Ab])
                kb.op(kb.act, nc.scalar.activation, out=A[:], in_=A[:], func=AF.Ln, reads=[Ab], writes=[Ab])
                kb.op(kb.dve, nc.vector.tensor_tensor_scan, out=C[:], data0=m01[:], data1=A[:], initial=0.0, op0=ALU.mult, op1=ALU.add,
                      reads=[m01b, Ab], begins=[Cb])
                if d == 1:
                    kb.op(kb.dve, nc.vector.tensor_copy, out=tot[:], in_=v3(C)[:, :, 63], reads=[Cb], begins=[totb])
                    kb.op(kb.dve, nc.vector.tensor_tensor, out=C[:], in0=A[:], in1=C[:], op=ALU.subtract, reads=[Ab, Cb], writes=[Cb])
                    kb.op(kb.dve, nc.vector.tensor_tensor, out=v3(C), in0=v3(C), in1=tot[:].unsqueeze(2).to_broadcast([128, NCH, 64]),
                          op=ALU.add, reads=[Cb, totb], writes=[Cb])
                mid = 31 if d == 0 else 32
                endi = 63 if d == 0 else 0
                kb.op(kb.dve, nc.vector.tensor_copy, out=ref[:], in_=v3(C)[:, :, mid], reads=[Cb], begins=[refb])
                kb.op(kb.dve, nc.vector.tensor_copy, out=cend[:], in_=v3(C)[:, :, endi], reads=[Cb], begins=[cendb])
                kb.op(kb.dve, nc.vector.tensor_tensor, out=v3(C), in0=v3(C), in1=ref[:].unsqueeze(2).to_broadcast([128, NCH, 64]),
                      op=ALU.subtract, reads=[Cb, refb], writes=[Cb])
                kb.op(kb.act, nc.scalar.activation, out=A[:], in_=C[:], func=AF.Exp, reads=[Cb], writes=[Ab])
                kb.op(kb.dve, nc.vector.tensor_tensor, out=qt[:], in0=qsil[:], in1=A[:], op=ALU.mult, reads=[qsilb, Ab], begins=[qtb])
                kb.op(kb.act, nc.scalar.activation, out=A[:], in_=C[:], func=AF.Exp, scale=-1.0, reads=[Cb, qtb], writes=[Ab])
                kb.op(kb.dve, nc.vector.tensor_tensor, out=kt[:], in0=Bk[:], in1=A[:], op=ALU.mult, reads=[Bkb, Ab], begins=[ktb])
                order = list(range(NCH)) if d == 0 else [3, 2, 1, 0] + list(range(NCH - 1, 3, -1))
                kb.op(kb.dve, nc.vector.tensor_tensor, out=gam[:], in0=cend[:], in1=ref[:], op=ALU.subtract, reads=[cendb, refb], begins=[gamb])
                if d == 0:
                    kb.op(kb.dve, nc.vector.tensor_tensor, out=gam[:, 0:NCH - 1], in0=gam[:, 0:NCH - 1], in1=ref[:, 1:NCH], op=ALU.add,
                          reads=[gamb, refb], writes=[gamb])
                else:
                    kb.op(kb.dve, nc.vector.tensor_tensor, out=gam[:, 1:NCH], in0=gam[:, 1:NCH], in1=ref[:, 0:NCH - 1], op=ALU.add,
                          reads=[gamb, refb], writes=[gamb])
                    kb.op(kb.dve, nc.vector.tensor_tensor, out=gam[:, 0:1], in0=gam[:, 0:1], in1=ref[:, NCH - 1:NCH], op=ALU.add,
                          reads=[gamb, refb], writes=[gamb])
                kb.op(kb.act, nc.scalar.activation, out=gam[:], in_=gam[:], func=AF.Exp, reads=[gamb], writes=[gamb])
                kb.op(kb.dve, nc.vector.tensor_tensor, out=v3(kh), in0=v3(kt), in1=gam[:].unsqueeze(2).to_broadcast([128, NCH, 64]),
                      op=ALU.mult, reads=[ktb, gamb], begins=[khb])
                for q4 in range(0, NT, 4):
                    pt, ptb = pst.next()
                    nn = min(4, NT - q4)
                    for j in range(nn):
                        tt = q4 + j
                        kb.op(kb.pe, nc.tensor.transpose, out=pt[:, j * 128:(j + 1) * 128], in_=kh[:, tt * 128:(tt + 1) * 128],
                              identity=cs.identb[:], reads=[khb, cs.identb_b], begins=[ptb] if j == 0 else (),
                              writes=[ptb] if j else (), signal=(j == nn - 1))
                    kb.op(kb.act, nc.scalar.activation, out=ktok[:, q4:q4 + nn, :], in_=pt[:, 0:nn * 128].rearrange("p (a b) -> p a b", b=128),
                          func=AF.Identity, scale=1.0, reads=[ptb], begins=[ktokb] if q4 == 0 else (), writes=[ktokb] if q4 else ())
                kb.op(kb.dve, nc.vector.memset, Sm[:], 0.0, begins=[Smb_])
                for c in order:
                    tt, r0 = c // 2, (c % 2) * 64
                    c0 = c * 64
                    lat = c >= 4
                    if lat:
                        Sb, Sbb = Sbp.next()
                        kb.op(kb.act, nc.scalar.activation, out=Sb[:], in_=Sm[:], func=AF.Identity, scale=1.0, reads=[Smb_], begins=[Sbb])
                        ps, psb = psS.next()
                        kb.op(kb.pe, nc.tensor.matmul, ps[r0:r0 + 64, 0:64], lhsT=kt[:, c0:c0 + 64], rhs=qt[:, c0:c0 + 64], start=True,
                              stop=True, reads=[ktb, qtb], begins=[psb])
                        sc, scb = scp.next()
                        kb.op(kb.dve, nc.vector.tensor_tensor, out=sc[r0:r0 + 64, :], in0=ps[r0:r0 + 64, 0:64], in1=mk[r0:r0 + 64, :],
                              op=ALU.mult, reads=[psb, mkb], begins=[scb])
                        po, pob = psO.next()
                        kb.op(kb.pe, nc.tensor.matmul, po[r0:r0 + 64, 0:128], lhsT=qt[:, c0:c0 + 64], rhs=Sb[:], start=True, stop=False,
                              reads=[qtb, Sbb], begins=[pob], signal=False)
                        kb.op(kb.pe, nc.tensor.matmul, po[r0:r0 + 64, 0:128], lhsT=sc[r0:r0 + 64, :], rhs=v[r0:r0 + 64, tt, :], start=False,
                              stop=True, reads=[scb, vb], writes=[pob])
                        if d == 0:
                            kb.op(kb.dve, nc.vector.tensor_copy, out=oacc[r0:r0 + 64, tt - 2, :], in_=po[r0:r0 + 64, 0:128], reads=[pob],
                                  begins=[oaccb] if c == 4 else (), writes=[oaccb] if c != 4 else ())
                        else:
                            kb.op(kb.dve, nc.vector.tensor_tensor, out=oacc[r0:r0 + 64, tt - 2, :], in0=oacc[r0:r0 + 64, tt - 2, :],
                                  in1=po[r0:r0 + 64, 0:128], op=ALU.add, reads=[pob, oaccb], writes=[oaccb])
                    pk, pkb = psK.next()
                    kb.op(kb.pe, nc.tensor.matmul, pk[:, 0:128], lhsT=ktok[r0:r0 + 64, tt, :], rhs=v[r0:r0 + 64, tt, :], start=True, stop=True,
                          reads=[ktokb, vb], begins=[pkb])
                    kb.op(kb.dve, nc.vector.scalar_tensor_tensor, out=Sm[:], in0=Sm[:], scalar=gam[:, c:c + 1], in1=pk[:, 0:128],
                          op0=ALU.mult, op1=ALU.add, reads=[Smb_, gamb, pkb], writes=[Smb_])
            for q4 in range(0, 16, 4):
                pt, ptb = pst.next()
                for j in range(4):
                    tl = q4 + j
                    kb.op(kb.act, nc.scalar.activation, out=junk[:], in_=oacc[:, tl, :], func=AF.Square, accum_out=ssq[:, tl:tl + 1],
                          reads=[oaccb], begins=[junkb, ssqb] if tl == 0 else [junkb], writes=[ssqb] if tl else ())
                    kb.op(kb.act, nc.scalar.activation, out=ssq[:, tl:tl + 1], in_=ssq[:, tl:tl + 1], func=AF.Sqrt, scale=1.0 / 128,
                          bias=cs.epsr[:], reads=[ssqb, cs.epsr_b], writes=[ssqb])
                    kb.op(kb.dve, nc.vector.reciprocal, out=ssq[:, tl:tl + 1], in_=ssq[:, tl:tl + 1], reads=[ssqb], writes=[ssqb])
                    kb.op(kb.dve, nc.vector.scalar_tensor_tensor, out=junk[:], in0=oacc[:, tl, :], scalar=ssq[:, tl:tl + 1], in1=NW[:],
                          op0=ALU.mult, op1=ALU.mult, reads=[oaccb, ssqb, NWb], writes=[junkb])
                    yt, ytb = ytp.next()
                    kb.op(kb.dve, nc.vector.tensor_tensor, out=yt[:], in0=junk[:], in1=gsil[:, tl, :], op=ALU.mult, reads=[junkb, gsilb],
                          begins=[ytb])
                    kb.op(kb.pe, nc.tensor.transpose, out=pt[:, j * 128:(j + 1) * 128], in_=yt[:], identity=cs.identb[:],
                          reads=[ytb, cs.identb_b], begins=[ptb] if j == 0 else (), writes=[ptb] if j else (), signal=True)
                yT, yTb = yTp.next()
                kb.op(kb.act, nc.scalar.activation, out=yT[:], in_=pt[:, 0:512], func=AF.Identity, scale=1.0, reads=[ptb], begins=[yTb])
                kb.dma(kb.sp, out=cat[h * 128:(h + 1) * 128, NCTX + q4 * 128:NCTX + (q4 + 4) * 128], in_=yT[:], sbuf_buf=yTb, reads=[yTb])
        kb.end_phase()


BPC = 2


def build_program(bpc=BPC):
    nc = bass.Bass("TRN2", target_bir_lowering=False)
    dr = {}

    def din(name, shape, dt=F32):
        dr[name] = nc.dram_tensor(name, shape, dt, kind="ExternalInput").ap()

    def dint(name, shape, dt=F32):
        dr[name] = nc.dram_tensor(name, shape, dt, kind="Internal").ap()

    din("x", [bpc, NLAT, D]); din("ctx", [bpc, NCTX, D]); din("c2", [bpc, 2, D])
    din("mod_w", [2, D, 6 * D]); din("mod_b", [2, 6 * D])
    din("ln_mix_g", [2, D]); din("ln_mix_b", [2, D]); din("ln_ffn_g", [2, D]); din("ln_ffn_b", [2, D])
    din("even_w_in", [D, 5120]); din("even_w_out", [D, D]); din("diff_lambda", [4, 64]); din("diff_subln", [128])
    din("hgrn_w_in", [D, 10240]); din("hgrn_w_out", [D, D]); din("hgrn_lower_bounds", [2, 2, D]); din("hgrn_norm", [128])
    din("ffn_w_up", [2, D, 2 * DFF]); din("ffn_conv_w", [2, 3, DFF]); din("ffn_conv_b", [2, DFF]); din("ffn_w_down", [2, DFF, D])
    din("rope_cos", [128, NLAT]); din("rope_sin", [128, NLAT]); din("dft128", [128, 256], BF16)
    din("dft256", [2, 256, 256], BF16); din("dft2048", [2, 2048, 2048], BF16)
    yout = nc.dram_tensor("y", [bpc, NLAT, D], F32, kind="ExternalOutput").ap()
    dint("modrow", [2, 2, 6 * D]); dint("cat0", [D, T], BF16); dint("cat1", [D, T], BF16)
    dint("hmid", [T, D]); dint("uf", [128, KC, UFW], BF16); dint("hres", [T, D])
    kb = KB(nc)
    all_tiles = list(range(NT))
    lat_tiles = list(range(2, NT))
    xs, ctxs, c2s = dr["x"], dr["ctx"], dr["c2"]

    with kb.stack:
        for b in range(bpc):
            dr["x"], dr["ctx"], dr["c2"] = xs[b], ctxs[b], c2s[b]
            kb.begin_batch()

            def hin0(tt, b=b):
                return ctxs[b][tt * 128:(tt + 1) * 128, :] if tt < 2 else xs[b][(tt - 2) * 128:(tt - 1) * 128, :]

            def hres_t(tt):
                return dr["hres"][tt * 128:(tt + 1) * 128, :]

            phase_mod(kb, dr)
            phase_even(kb, dr)
            phase_post(kb, dr, 0, dr["cat0"], dr["even_w_out"], hin0, all_tiles, dr["ln_mix_g"][0], dr["ln_mix_b"][0])
            phase_ffn(kb, dr, 0, all_tiles, hres_t, dr["ln_ffn_g"][0], dr["ln_ffn_b"][0])
            phase_hgrn(kb, dr)
            phase_post(kb, dr, 1, dr["cat1"], dr["hgrn_w_out"], hres_t, lat_tiles, dr["ln_mix_g"][1], dr["ln_mix_b"][1])
            phase_ffn(kb, dr, 1, lat_tiles, lambda tt, b=b: yout[b][(tt - 2) * 128:(tt - 1) * 128, :], dr["ln_ffn_g"][1], dr["ln_ffn_b"][1])
    print("semaphores used:", kb.nsem)
    return nc


def kernel(x, c, ctx, c_ctx, mod_w, mod_b, ln_mix_g, ln_mix_b, ln_ffn_g, ln_ffn_b, even_w_in, even_w_out, diff_lambda,
           diff_subln, hgrn_w_in, hgrn_w_out, hgrn_lower_bounds, hgrn_norm, ffn_w_up, ffn_conv_w, ffn_conv_b, ffn_w_down):
    f = lambda a: np.ascontiguousarray(np.asarray(a, dtype=np.float32))
    tabs = host_tables()
    shared = {
        "mod_w": f(mod_w), "mod_b": f(mod_b), "ln_mix_g": f(ln_mix_g), "ln_mix_b": f(ln_mix_b), "ln_ffn_g": f(ln_ffn_g),
        "ln_ffn_b": f(ln_ffn_b), "even_w_in": f(even_w_in)[0], "even_w_out": f(even_w_out)[0], "diff_lambda": f(diff_lambda)[0],
        "diff_subln": f(diff_subln)[0], "hgrn_w_in": f(hgrn_w_in)[0], "hgrn_w_out": f(hgrn_w_out)[0],
        "hgrn_lower_bounds": f(hgrn_lower_bounds), "hgrn_norm": f(hgrn_norm)[0], "ffn_w_up": f(ffn_w_up), "ffn_conv_w": f(ffn_conv_w),
        "ffn_conv_b": f(ffn_conv_b), "ffn_w_down": f(ffn_w_down), **tabs,
    }
    x = f(x); ctx = f(ctx); c = f(c); c_ctx = f(c_ctx)
    nb = x.shape[0]
    ncores = nb // BPC
    in_maps = []
    for i in range(ncores):
        m = dict(shared)
        bs = list(range(i * BPC, (i + 1) * BPC))
        m["x"] = np.ascontiguousarray(x[bs])
        m["ctx"] = np.ascontiguousarray(ctx[bs])
        m["c2"] = np.ascontiguousarray(np.stack([np.stack([c[b], c_ctx]) for b in bs]))
        in_maps.append(m)
    nc = build_program()
    res = run_bass_kernel_spmd(nc, in_maps, core_ids=list(range(ncores)))
    return np.concatenate([np.asarray(r["y"], dtype=np.float32) for r in res.results], axis=0)
```

```python
import contextlib
import math
import numpy as np
import ml_dtypes
import concourse.bass as bass
import concourse.mybir as mybir
from concourse.bass_utils import run_bass_kernel_spmd

F32 = mybir.dt.float32
BF16 = mybir.dt.bfloat16
AF = mybir.ActivationFunctionType
ALU = mybir.AluOpType
AX = mybir.AxisListType

D = 2048
NCTX = 256
NLAT = 2048
T = NCTX + NLAT
NT = T // 128
KC = D // 128
DFF = 5504
FC = DFF // 128
ALPHA = (2.0 * 2) ** 0.25
LN_EPS = 1e-6
RMS_EPS = 1e-5
SELF_SYNC = True


class TokSet:
    __slots__ = ("d",)

    def __init__(self):
        self.d = {}

    def add(self, tok):
        if tok is None:
            return
        sem, val = tok
        k = id(sem)
        cur = self.d.get(k)
        if cur is None or cur[1] < val:
            self.d[k] = (sem, val)

    def update(self, other):
        for t in other.d.values():
            self.add(t)

    def toks(self):
        return list(self.d.values())


class Buf:
    def __init__(self, name=""):
        self.name = name
        self.writers = TokSet()
        self.readers = TokSet()
        self.war = TokSet()
        self.dsem = None

    def begin(self):
        w = TokSet()
        w.update(self.writers)
        w.update(self.readers)
        self.war = w
        self.writers = TokSet()
        self.readers = TokSet()


class Eng:
    def __init__(self, k, name, eng):
        self.k = k
        self.name = name
        self.eng = eng
        self.sem = None
        self.count = 0
        self.waited = {}
        self.pending_unsignaled = False

    def wait(self, tok):
        sem, val = tok
        if sem is self.sem:
            if self.name == "pe" or not SELF_SYNC:
                return
        k = id(sem)
        if self.waited.get(k, 0) >= val:
            return
        self.waited[k] = val
        self.eng.wait_ge(sem, val)


class KB:
    def __init__(self, nc):
        self.nc = nc
        self.pe = Eng(self, "pe", nc.tensor)
        self.act = Eng(self, "act", nc.scalar)
        self.dve = Eng(self, "dve", nc.vector)
        self.pool = Eng(self, "pool", nc.gpsimd)
        self.sp = Eng(self, "sp", nc.sync)
        self.engs = [self.pe, self.act, self.dve, self.pool, self.sp]
        self.stack = contextlib.ExitStack()
        self.dma_sems = []
        self.dma_free = []
        self.phase_bufs = []
        self.nsem = 0

    def new_sem(self, name):
        self.nsem += 1
        return self.stack.enter_context(self.nc.semaphore(f"{name}_{self.nsem}"))

    def begin_batch(self):
        self.slot = 0

    def start_phase(self):
        if not hasattr(self, "slot"):
            self.slot = 0
            self.slot_sems = {}
        if not hasattr(self, "slot_sems"):
            self.slot_sems = {}
        self.cur_slot = self.slot
        self.slot += 1
        saved = self.slot_sems.get(self.cur_slot)
        for e in self.engs:
            if e.name == "sp":
                continue
            if saved is None:
                e.sem = self.new_sem("e" + e.name)
                e.count = 0
            else:
                e.sem, e.count = saved[e.name]

    def get_dsem(self, buf):
        if buf.dsem is None:
            if self.dma_free:
                buf.dsem = self.dma_free.pop()
            else:
                buf.dsem = [self.new_sem("d"), 0]
                self.dma_sems.append(buf.dsem)
            self.phase_bufs.append(buf)
        return buf.dsem

    def barrier(self):
        toks = []
        for e in self.engs:
            if e.sem is not None and e.count > 0:
                toks.append((e.sem, e.count))
        for ds in self.dma_sems:
            if ds[1] > 0:
                toks.append((ds[0], ds[1]))
        for e in self.engs:
            for t in toks:
                if t[0] is e.sem:
                    continue
                e.wait(t)

    def end_phase(self):
        self.barrier()
        self.slot_sems[self.cur_slot] = {e.name: (e.sem, e.count) for e in self.engs if e.name != "sp"}
        for b in self.phase_bufs:
            if b.dsem is not None:
                self.dma_free.append(b.dsem)
                b.dsem = None
        self.phase_bufs = []

    def _deps(self, e, reads, writes, begins, deps):
        ts = TokSet()
        for b in begins:
            b.begin()
            ts.update(b.war)
        for b in writes:
            ts.update(b.war)
            ts.update(b.readers)
        for b in reads:
            ts.update(b.writers)
        for t in deps:
            ts.add(t)
        for t in ts.toks():
            e.wait(t)

    def op(self, e, fn, *args, reads=(), writes=(), begins=(), deps=(), signal=True, **kw):
        self._deps(e, reads, writes, begins, deps)
        ins = fn(*args, **kw)
        if signal:
            e.count += 1
            ins.then_inc(e.sem, 1)
            tok = (e.sem, e.count)
        else:
            tok = (e.sem, e.count + 1)
        for b in reads:
            b.readers.add(tok)
        for b in writes:
            b.writers.add(tok)
        for b in begins:
            b.writers.add(tok)
        return tok

    def dma(self, e, out, in_, sbuf_buf, reads=(), writes=(), begins=(), deps=(), **kw):
        self._deps(e, reads, writes, begins, deps)
        ds = self.get_dsem(sbuf_buf)
        ds[1] += 16
        e.eng.dma_start(out=out, in_=in_, **kw).then_inc(ds[0], 16)
        tok = (ds[0], ds[1])
        for b in reads:
            b.readers.add(tok)
        for b in writes:
            b.writers.add(tok)
        for b in begins:
            b.writers.add(tok)
        return tok


class Pool:
    def __init__(self, kb, es, name, shape, dtype, n, psum=False):
        self.tiles = []
        for i in range(n):
            if psum:
                t = es.enter_context(kb.nc.psum_tensor(uname(f"{name}{i}"), shape, dtype))
            else:
                t = es.enter_context(kb.nc.sbuf_tensor(uname(f"{name}{i}"), shape, dtype))
            self.tiles.append((t, Buf(f"{name}{i}")))
        self.i = 0

    def next(self):
        r = self.tiles[self.i % len(self.tiles)]
        self.i += 1
        return r


_UID = [0]


def uname(name):
    _UID[0] += 1
    return f"{name}_u{_UID[0]}"


def sb(kb, es, name, shape, dtype):
    t = es.enter_context(kb.nc.sbuf_tensor(uname(name), shape, dtype))
    return t, Buf(name)


class Res:
    pass


def make_consts(kb, es):
    nc = kb.nc
    r = Res()
    r.identb, r.identb_b = sb(kb, es, "identb", [128, 128], BF16)
    r.identf, r.identf_b = sb(kb, es, "identf", [128, 128], F32)
    r.eps, r.eps_b = sb(kb, es, "epsln", [128, 1], F32)
    r.epsr, r.epsr_b = sb(kb, es, "epsrms", [128, 1], F32)
    r.onesb, r.onesb_b = sb(kb, es, "onesb", [128, 128], BF16)
    r.onesf, r.onesf_b = sb(kb, es, "onesf", [128, 128], F32)
    kb.op(kb.pool, nc.gpsimd.memset, r.identb[:], 1.0, begins=[r.identb_b])
    kb.op(kb.pool, nc.gpsimd.affine_select, out=r.identb[:], in_=r.identb[:], pattern=[[-1, 128]],
          compare_op=ALU.is_equal, fill=0.0, base=0, channel_multiplier=1,
          reads=[r.identb_b], writes=[r.identb_b])
    kb.op(kb.pool, nc.gpsimd.memset, r.identf[:], 1.0, begins=[r.identf_b])
    kb.op(kb.pool, nc.gpsimd.affine_select, out=r.identf[:], in_=r.identf[:], pattern=[[-1, 128]],
          compare_op=ALU.is_equal, fill=0.0, base=0, channel_multiplier=1,
          reads=[r.identf_b], writes=[r.identf_b])
    kb.op(kb.pool, nc.gpsimd.memset, r.eps[:], LN_EPS, begins=[r.eps_b])
    kb.op(kb.pool, nc.gpsimd.memset, r.epsr[:], RMS_EPS, begins=[r.epsr_b])
    kb.op(kb.pool, nc.gpsimd.memset, r.onesb[:], 1.0, begins=[r.onesb_b])
    kb.op(kb.pool, nc.gpsimd.memset, r.onesf[:], 1.0, begins=[r.onesf_b])
    return r


def load_colT(kb, es, cs, name, row_ap, n, pst):
    nc = kb.nc
    rt, rtb = sb(kb, es, name + "r", [n, 128], F32)
    kb.dma(kb.sp, out=rt[:], in_=row_ap.rearrange("(j p) -> j p", p=128), sbuf_buf=rtb, begins=[rtb])
    t, tb = sb(kb, es, name, [128, n], F32)
    ps, psb = pst.next()
    kb.op(kb.pe, nc.tensor.transpose, out=ps[:, 0:n], in_=rt[:, :], identity=cs.identf[0:n, 0:n],
          reads=[rtb, cs.identf_b], begins=[psb])
    kb.op(kb.dve, nc.vector.tensor_copy, out=t[:], in_=ps[:, 0:n], reads=[psb], begins=[tb])
    return t, tb


def load_modT(kb, es, cs, dr, l, name, pst):
    nc = kb.nc
    outs = []
    for r in range(2):
        m, mb = load_colT(kb, es, cs, f"{name}{r}", dr["modrow"][l, r], 96, pst)
        for j0 in (16, 64):
            kb.op(kb.dve, nc.vector.tensor_scalar, out=m[:, j0:j0 + 16], in0=m[:, j0:j0 + 16], scalar1=1.0,
                  scalar2=None, op0=ALU.add, reads=[mb], writes=[mb])
        outs.append((m, mb))
    return outs


def load_bcast(kb, es, name, row_ap, n, eng=None):
    t, b = sb(kb, es, name, [128, n], F32)
    kb.dma(eng or kb.sp, out=t[:], in_=row_ap.partition_broadcast(128), sbuf_buf=b, begins=[b])
    return t, b


class LNRes:
    def __init__(self, kb, es, tag=""):
        self.stats = Pool(kb, es, "lnst" + tag, [128, 4, 6], F32, 2)
        self.mv = Pool(kb, es, "lnmv" + tag, [128, 2], F32, 2)
        self.rstd = Pool(kb, es, "lnrs" + tag, [128, 1], F32, 2)
        self.nmr = Pool(kb, es, "lnnm" + tag, [128, 1], F32, 2)


def emit_ln_stats(kb, cs, lr, x, xb):
    nc = kb.nc
    st, stb = lr.stats.next()
    for j in range(4):
        kb.op(kb.dve, nc.vector.bn_stats, out=st[:, j, :], in_=x[:, j * 512:(j + 1) * 512], reads=[xb],
              begins=[stb] if j == 0 else (), writes=[stb] if j else ())
    mv, mvb = lr.mv.next()
    kb.op(kb.dve, nc.vector.bn_aggr, out=mv[:], in_=st[:], reads=[stb], begins=[mvb])
    rs, rsb = lr.rstd.next()
    kb.op(kb.act, nc.scalar.activation, out=rs[:], in_=mv[:, 1:2], func=AF.Sqrt, bias=cs.eps[:], scale=1.0,
          reads=[mvb, cs.eps_b], begins=[rsb])
    kb.op(kb.dve, nc.vector.reciprocal, out=rs[:], in_=rs[:], reads=[rsb], writes=[rsb])
    nm, nmb = lr.nmr.next()
    kb.op(kb.dve, nc.vector.tensor_scalar, out=nm[:], in0=mv[:, 0:1], scalar1=rs[:, 0:1], scalar2=-1.0,
          op0=ALU.mult, op1=ALU.mult, reads=[mvb, rsb], begins=[nmb])
    return rs, rsb, nm, nmb


def emit_ln_to_uT(kb, cs, lr, x, xb, nbpool, pstp, scaleT, shiftT, modb, uT, uTb, col0, first_write):
    nc = kb.nc
    rs, rsb, nm, nmb = emit_ln_stats(kb, cs, lr, x, xb)
    nb, nbb = nbpool.next()
    kb.op(kb.act, nc.scalar.activation, out=nb[:], in_=x[:], func=AF.Identity, scale=rs[:, 0:1], bias=nm[:, 0:1],
          reads=[xb, rsb, nmb], begins=[nbb])
    for q in range(4):
        pt, ptb = pstp.next()
        for j in range(4):
            kc = q * 4 + j
            kb.op(kb.pe, nc.tensor.transpose, out=pt[:, j * 128:(j + 1) * 128], in_=nb[:, kc * 128:(kc + 1) * 128],
                  identity=cs.identb[:], reads=[nbb, cs.identb_b], begins=[ptb] if j == 0 else (),
                  writes=[ptb] if j else (), signal=(j == 3))
        for j in range(4):
            kc = q * 4 + j
            beg = first_write and kc == 0
            if q % 2 == 0:
                kb.op(kb.act, nc.scalar.activation, out=uT[:, kc, col0:col0 + 128], in_=pt[:, j * 128:(j + 1) * 128],
                      func=AF.Identity, scale=scaleT[:, kc:kc + 1], bias=shiftT[:, kc:kc + 1],
                      reads=[ptb, modb], begins=[uTb] if beg else (), writes=() if beg else [uTb])
            else:
                kb.op(kb.dve, nc.vector.tensor_scalar, out=uT[:, kc, col0:col0 + 128], in0=pt[:, j * 128:(j + 1) * 128],
                      scalar1=scaleT[:, kc:kc + 1], scalar2=shiftT[:, kc:kc + 1], op0=ALU.mult, op1=ALU.add,
                      reads=[ptb, modb], writes=[uTb])


def phase_mod(kb, dr):
    nc = kb.nc
    with contextlib.ExitStack() as es:
        kb.start_phase()
        cT, cTb = sb(kb, es, "cT", [128, 2, 16], F32)
        sT, sTb = sb(kb, es, "sT", [128, 16, 2], BF16)
        for r in range(2):
            kb.dma(kb.sp, out=cT[:, r, :], in_=dr["c2"][r].rearrange("(kc p) -> p kc", p=128), sbuf_buf=cTb,
                   begins=[cTb] if r == 0 else (), writes=[cTb] if r else (), allow_slow_non_contiguous=True)
        for r in range(2):
            kb.op(kb.act, nc.scalar.activation, out=sT[:, :, r], in_=cT[:, r, :], func=AF.Silu, reads=[cTb],
                  begins=[sTb] if r == 0 else (), writes=[sTb] if r else ())
        wpool = Pool(kb, es, "mw", [128, 16, 512], BF16, 3)
        pspool = Pool(kb, es, "mps", [2, 512], F32, 2, psum=True)
        bias, biasb = sb(kb, es, "mbias", [2, 6 * D], F32)
        row, rowb = sb(kb, es, "mrow", [2, 6 * D], F32)
        for l in range(2):
            kb.dma(kb.sp, out=bias[:], in_=dr["mod_b"][l].partition_broadcast(2), sbuf_buf=biasb, begins=[biasb])
            for cg in range(24):
                w, wb = wpool.next()
                kb.dma(kb.pool, out=w[:], in_=dr["mod_w"][l, :, cg * 512:(cg + 1) * 512].rearrange("(kc p) n -> p kc n", p=128),
                       sbuf_buf=wb, begins=[wb])
                ps, psb = pspool.next()
                for kc in range(16):
                    kb.op(kb.pe, nc.tensor.matmul, ps[:], lhsT=sT[:, kc, :], rhs=w[:, kc, :], start=(kc == 0),
                          stop=(kc == 15), reads=[sTb, wb], begins=[psb] if kc == 0 else (),
                          writes=[psb] if kc else (), signal=(kc == 15))
                kb.op(kb.dve, nc.vector.tensor_tensor, out=row[:, cg * 512:(cg + 1) * 512], in0=ps[:],
                      in1=bias[:, cg * 512:(cg + 1) * 512], op=ALU.add, reads=[psb, biasb],
                      begins=[rowb] if cg == 0 else (), writes=[rowb] if cg else ())
            kb.dma(kb.sp, out=dr["modrow"][l], in_=row[:], sbuf_buf=rowb, reads=[rowb])
        kb.end_phase()


UFW = 2308


def ufcol(tok):
    return 1 + tok if tok < NCTX else 259 + (tok - NCTX)


def phase_post(kb, dr, l, cat_ap, wout_ap, hin_fn, tiles, lng, lnb):
    nc = kb.nc
    with contextlib.ExitStack() as es:
        kb.start_phase()
        cs = make_consts(kb, es)
        lr = LNRes(kb, es)
        psm = Pool(kb, es, "psm", [128, 512], F32, 3, psum=True)
        pst = Pool(kb, es, "pst", [128, 1024], BF16, 2, psum=True)
        mods = load_modT(kb, es, cs, dr, l, "modT", psm)
        G, Gb = load_bcast(kb, es, "lnG", lng, D)
        Bt, Bb = load_bcast(kb, es, "lnB", lnb, D)
        gates = {}
        for r in sorted({0 if tt >= 2 else 1 for tt in tiles}):
            gates[r] = load_bcast(kb, es, f"gate{r}", dr["modrow"][l, r, 2 * D:3 * D], D)
        w, wb = sb(kb, es, "wout", [128, KC, D], BF16)
        wv = wout_ap.rearrange("(kc p) n -> p kc n", p=128)
        for q in range(4):
            kb.dma(kb.pool, out=w[:, q * 4:(q + 1) * 4, :], in_=wv[:, q * 4:(q + 1) * 4, :], sbuf_buf=wb,
                   begins=[wb] if q == 0 else (), writes=[wb] if q else ())
        ufv = dr["uf"]
        catp = Pool(kb, es, "catt", [128, KC, 128], BF16, 2)
        hp = Pool(kb, es, "hin", [128, D], F32, 2)
        zp = Pool(kb, es, "zt", [128, D], F32, 2)
        tmpp = Pool(kb, es, "ztmp", [128, 512], F32, 2)
        nbp = Pool(kb, es, "nb", [128, D], BF16, 2)
        ufp = Pool(kb, es, "uft", [128, KC, 128], BF16, 2)
        catv = cat_ap.rearrange("(kc p) t -> p kc t", p=128)
        for tt in tiles:
            r = 0 if tt >= 2 else 1
            gt, gtb = gates[r]
            ct, ctb = catp.next()
            kb.dma(kb.sp, out=ct[:], in_=catv[:, :, tt * 128:(tt + 1) * 128], sbuf_buf=ctb, begins=[ctb])
            h, hb = hp.next()
            kb.dma(kb.sp, out=h[:], in_=hin_fn(tt), sbuf_buf=hb, begins=[hb])
            z, zb = zp.next()
            for cg in range(4):
                ps, psb = psm.next()
                for kc in range(KC):
                    kb.op(kb.pe, nc.tensor.matmul, ps[:], lhsT=ct[:, kc, :], rhs=w[:, kc, cg * 512:(cg + 1) * 512],
                          start=(kc == 0), stop=(kc == KC - 1), reads=[ctb, wb], begins=[psb] if kc == 0 else (),
                          writes=[psb] if kc else (), signal=(kc == KC - 1))
                tm, tmb = tmpp.next()
                kb.op(kb.dve, nc.vector.tensor_tensor, out=tm[:], in0=ps[:], in1=gt[:, cg * 512:(cg + 1) * 512],
                      op=ALU.mult, reads=[psb, gtb], begins=[tmb])
                kb.op(kb.dve, nc.vector.scalar_tensor_tensor, out=z[:, cg * 512:(cg + 1) * 512],
                      in0=h[:, cg * 512:(cg + 1) * 512], scalar=ALPHA, in1=tm[:], op0=ALU.mult, op1=ALU.add,
                      reads=[hb, tmb], begins=[zb] if cg == 0 else (), writes=[zb] if cg else ())
            rs, rsb, nm, nmb = emit_ln_stats(kb, cs, lr, z, zb)
            kb.op(kb.act, nc.scalar.activation, out=z[:], in_=z[:], func=AF.Identity, scale=rs[:, 0:1], bias=nm[:, 0:1],
                  reads=[zb, rsb, nmb], writes=[zb])
            kb.op(kb.pool, nc.gpsimd.tensor_tensor, out=z[:], in0=z[:], in1=G[:], op=ALU.mult, reads=[zb, Gb], writes=[zb])
            kb.op(kb.pool, nc.gpsimd.tensor_tensor, out=z[:], in0=z[:], in1=Bt[:], op=ALU.add, reads=[zb, Bb], writes=[zb])
            kb.dma(kb.sp, out=dr["hmid"][tt * 128:(tt + 1) * 128, :], in_=z[:], sbuf_buf=zb, reads=[zb])
            uft, uftb = ufp.next()
            emit_ln_to_uT(kb, cs, lr, z, zb, nbp, pst, mods[r][0][:, 64:80], mods[r][0][:, 48:64], mods[r][1], uft, uftb, 0, True)
            c0 = ufcol(tt * 128)
            kb.dma(kb.sp, out=ufv[:, :, c0:c0 + 128], in_=uft[:], sbuf_buf=uftb, reads=[uftb])
        kb.end_phase()


def ffn_groups(tiles):
    groups = []
    i = 0
    while i < len(tiles):
        g = tiles[i:i + 6]
        i += 6
        groups.append(g)
    out = []
    for g in groups:
        blocks = []
        runs = []
        for tt in g:
            seq = 0 if tt < 2 else 1
            if runs and runs[-1][0] == seq and runs[-1][2] == tt:
                runs[-1][2] = tt + 1
            else:
                runs.append([seq, tt, tt + 1])
        for seq, a, b in runs:
            ntok = (b - a) * 128
            c = ufcol(a * 128)
            nblk = (ntok + 383) // 384
            per = ntok // nblk
            assert per * nblk == ntok
            for j in range(nblk):
                blocks.append((c + j * per - 1, per + 2, (a - g[0]) * 128 + j * per))
        lo = min(bk[0] for bk in blocks)
        hi = max(bk[0] + bk[1] for bk in blocks)
        out.append((g, lo, hi, blocks))
    return out


def phase_ffn(kb, dr, l, tiles, hout_fn, lng, lnb):
    nc = kb.nc
    wup = dr["ffn_w_up"][l].rearrange("(kc p) n -> p kc n", p=128)
    wdn = dr["ffn_w_down"][l].rearrange("(fc p) n -> p fc n", p=128)
    with contextlib.ExitStack() as es:
        kb.start_phase()
        cs = make_consts(kb, es)
        lr = LNRes(kb, es)
        psa = Pool(kb, es, "psa", [128, 512], F32, 2, psum=True)
        psv = Pool(kb, es, "psv", [128, 512], F32, 2, psum=True)
        psd = Pool(kb, es, "psd", [128, 512], F32, 2, psum=True)
        pstr = Pool(kb, es, "pstr", [128, 512], F32, 2, psum=True)
        mods = load_modT(kb, es, cs, dr, l, "modT", pstr)
        G, Gb = load_bcast(kb, es, "lnG", lng, D)
        Bt, Bb = load_bcast(kb, es, "lnB", lnb, D)
        cws = [load_colT(kb, es, cs, f"convw{tap}", dr["ffn_conv_w"][l, tap], FC, pstr) for tap in range(3)]
        cbias, cbb = load_colT(kb, es, cs, "convb", dr["ffn_conv_b"][l], FC, pstr)
        hT, hTb = sb(kb, es, "hT", [128, FC, 768], BF16)
        for (g, lo, hi, blocks) in ffn_groups(tiles):
            ntok = len(g) * 128
            ncols = hi - lo
            with contextlib.ExitStack() as es2:
                ug, ugb = sb(kb, es2, "ug", [128, KC, 772], BF16)
                kb.dma(kb.sp, out=ug[:, :, 0:ncols], in_=dr["uf"][:, :, lo:hi], sbuf_buf=ugb, begins=[ugb])
                for pc in (0, 257, 258, 2307):
                    if lo <= pc < hi:
                        kb.op(kb.dve, nc.vector.memset, ug[:, :, pc - lo:pc - lo + 1], 0.0, reads=[ugb], writes=[ugb])
                wup_p = Pool(kb, es2, "wu", [128, KC, 2, 128], BF16, 3)
                c1p = Pool(kb, es2, "c1", [128, 386], F32, 3)
                c2p = Pool(kb, es2, "c2", [128, 386], F32, 3)
                for fc in range(FC):
                    wu, wub = wup_p.next()
                    kb.dma(kb.pool, out=wu[:, :, 0, :], in_=wup[:, :, fc * 128:(fc + 1) * 128], sbuf_buf=wub, begins=[wub])
                    kb.dma(kb.pool, out=wu[:, :, 1, :], in_=wup[:, :, DFF + fc * 128:DFF + (fc + 1) * 128], sbuf_buf=wub,
                           writes=[wub])
                    for (c0, n, t0) in blocks:
                        cc = c0 - lo
                        pa, pab = psa.next()
                        for kc in range(KC):
                            kb.op(kb.pe, nc.tensor.matmul, pa[:, 0:n], lhsT=wu[:, kc, 0, :], rhs=ug[:, kc, cc:cc + n],
                                  start=(kc == 0), stop=(kc == KC - 1), reads=[wub, ugb], begins=[pab] if kc == 0 else (),
                                  writes=[pab] if kc else (), signal=(kc == KC - 1))
                        pv, pvb = psv.next()
                        for kc in range(KC):
                            kb.op(kb.pe, nc.tensor.matmul, pv[:, 0:n - 2], lhsT=wu[:, kc, 1, :], rhs=ug[:, kc, cc + 1:cc + n - 1],
                                  start=(kc == 0), stop=(kc == KC - 1), reads=[wub, ugb], begins=[pvb] if kc == 0 else (),
                                  writes=[pvb] if kc else (), signal=(kc == KC - 1))
                        m = n - 2
                        c1, c1b = c1p.next()
                        kb.op(kb.act, nc.scalar.activation, out=c1[:, 0:m], in_=pa[:, 1:n - 1], func=AF.Identity,
                              scale=cws[1][0][:, fc:fc + 1], bias=cbias[:, fc:fc + 1], reads=[pab, cws[1][1], cbb], begins=[c1b])
                        c2, c2b = c2p.next()
                        kb.op(kb.dve, nc.vector.scalar_tensor_tensor, out=c2[:, 0:m], in0=pa[:, 0:m], scalar=cws[0][0][:, fc:fc + 1],
                              in1=c1[:, 0:m], op0=ALU.mult, op1=ALU.add, reads=[pab, cws[0][1], c1b], begins=[c2b])
                        kb.op(kb.dve, nc.vector.scalar_tensor_tensor, out=c1[:, 0:m], in0=pa[:, 2:n], scalar=cws[2][0][:, fc:fc + 1],
                              in1=c2[:, 0:m], op0=ALU.mult, op1=ALU.add, reads=[pab, cws[2][1], c2b], writes=[c1b])
                        kb.op(kb.act, nc.scalar.activation, out=c2[:, 0:m], in_=c1[:, 0:m], func=AF.Gelu, reads=[c1b], writes=[c2b])
                        kb.op(kb.dve, nc.vector.tensor_tensor, out=hT[:, fc, t0:t0 + m], in0=c2[:, 0:m], in1=pv[:, 0:m], op=ALU.mult,
                              reads=[c2b, pvb], begins=[hTb] if (fc == 0 and t0 == 0) else (),
                              writes=() if (fc == 0 and t0 == 0) else [hTb])
                kb.barrier()
            with contextlib.ExitStack() as es2:
                wdp = Pool(kb, es2, "wd", [128, FC, 128], BF16, 3)
                zs = [sb(kb, es2, f"z{i}", [128, D], F32) for i in range(len(g))]
                fgp = Pool(kb, es2, "fg", [128, 384], F32, 3)
                for i, tt in enumerate(g):
                    kb.dma(kb.sp, out=zs[i][0][:], in_=dr["hmid"][tt * 128:(tt + 1) * 128, :], sbuf_buf=zs[i][1], begins=[zs[i][1]])
                r_of = [0 if tt >= 2 else 1 for tt in g]
                nblk = [(b0, min(384, ntok - b0)) for b0 in range(0, ntok, 384)]
                for oc in range(KC):
                    wd, wdb = wdp.next()
                    kb.dma(kb.pool, out=wd[:], in_=wdn[:, :, oc * 128:(oc + 1) * 128], sbuf_buf=wdb, begins=[wdb])
                    for (b0, bn) in nblk:
                        pd, pdb = psd.next()
                        for fc in range(FC):
                            kb.op(kb.pe, nc.tensor.matmul, pd[:, 0:bn], lhsT=wd[:, fc, :], rhs=hT[:, fc, b0:b0 + bn],
                                  start=(fc == 0), stop=(fc == FC - 1), reads=[wdb, hTb], begins=[pdb] if fc == 0 else (),
                                  writes=[pdb] if fc else (), signal=(fc == FC - 1))
                        fg, fgb = fgp.next()
                        first = True
                        for j in range(bn // 128):
                            i = (b0 // 128) + j
                            kb.op(kb.act, nc.scalar.activation, out=fg[:, j * 128:(j + 1) * 128], in_=pd[:, j * 128:(j + 1) * 128],
                                  func=AF.Identity, scale=mods[r_of[i]][0][:, 80 + oc:81 + oc], reads=[pdb, mods[r_of[i]][1]],
                                  begins=[fgb] if first else (), writes=() if first else [fgb])
                            first = False
                        ptr, ptrb = pstr.next()
                        for j in range(bn // 128):
                            kb.op(kb.pe, nc.tensor.transpose, out=ptr[:, j * 128:(j + 1) * 128], in_=fg[:, j * 128:(j + 1) * 128],
                                  identity=cs.identf[:], reads=[fgb, cs.identf_b], begins=[ptrb] if j == 0 else (),
                                  writes=[ptrb] if j else (), signal=(j == bn // 128 - 1))
                        for j in range(bn // 128):
                            i = (b0 // 128) + j
                            z, zb = zs[i]
                            kb.op(kb.dve, nc.vector.scalar_tensor_tensor, out=z[:, oc * 128:(oc + 1) * 128],
                                  in0=z[:, oc * 128:(oc + 1) * 128], scalar=ALPHA, in1=ptr[:, j * 128:(j + 1) * 128],
                                  op0=ALU.mult, op1=ALU.add, reads=[ptrb, zb], writes=[zb])
                for i, tt in enumerate(g):
                    z, zb = zs[i]
                    rs, rsb, nm, nmb = emit_ln_stats(kb, cs, lr, z, zb)
                    kb.op(kb.act, nc.scalar.activation, out=z[:], in_=z[:], func=AF.Identity, scale=rs[:, 0:1], bias=nm[:, 0:1],
                          reads=[zb, rsb, nmb], writes=[zb])
                    kb.op(kb.pool, nc.gpsimd.tensor_tensor, out=z[:], in0=z[:], in1=G[:], op=ALU.mult, reads=[zb, Gb], writes=[zb])
                    kb.op(kb.pool, nc.gpsimd.tensor_tensor, out=z[:], in0=z[:], in1=Bt[:], op=ALU.add, reads=[zb, Bb], writes=[zb])
                    kb.dma(kb.sp, out=hout_fn(tt), in_=z[:], sbuf_buf=zb, reads=[zb])
                kb.barrier()
        kb.end_phase()


LAM_INIT0 = 0.8 - 0.6 * math.exp(-0.3 * 0)


def phase_even(kb, dr, heads=range(12), do_fourier=True):
    nc = kb.nc
    win = dr["even_w_in"].rearrange("(kc p) n -> p kc n", p=128)
    cat = dr["cat0"]
    with contextlib.ExitStack() as es:
        kb.start_phase()
        cs = make_consts(kb, es)
        lr = LNRes(kb, es)
        psA = Pool(kb, es, "psA", [128, 512], F32, 2, psum=True)
        psS = Pool(kb, es, "psS", [128, 512], F32, 2, psum=True)
        psO = Pool(kb, es, "psO", [128, 512], F32, 1, psum=True)
        psL = Pool(kb, es, "psL", [128, 512], F32, 1, psum=True)
        pst = Pool(kb, es, "pst", [128, 1024], BF16, 2, psum=True)
        mods = load_modT(kb, es, cs, dr, 0, "modT", psA)
        uT, uTb = sb(kb, es, "uT", [128, KC, T], BF16)
        with contextlib.ExitStack() as es2:
            xp = Pool(kb, es2, "xin", [128, D], F32, 2)
            nbp = Pool(kb, es2, "nb", [128, D], BF16, 2)
            for tt in range(NT):
                r = 0 if tt >= 2 else 1
                x, xb = xp.next()
                src = dr["ctx"][tt * 128:(tt + 1) * 128, :] if tt < 2 else dr["x"][(tt - 2) * 128:(tt - 1) * 128, :]
                kb.dma(kb.sp, out=x[:], in_=src, sbuf_buf=xb, begins=[xb])
                emit_ln_to_uT(kb, cs, lr, x, xb, nbp, pst, mods[r][0][:, 16:32], mods[r][0][:, 0:16], mods[r][1], uT, uTb,
                              tt * 128, tt == 0)
            kb.barrier()
        lv, lvb = load_bcast(kb, es, "lamv", dr["diff_lambda"].rearrange("a b -> (a b)"), 256)
        lam, lamb = sb(kb, es, "lam", [128, 4], F32)
        lt, ltb = sb(kb, es, "lamt", [128, 2, 64], F32)
        kb.op(kb.dve, nc.vector.tensor_tensor, out=lt[:, 0, :], in0=lv[:, 0:64], in1=lv[:, 64:128], op=ALU.mult, reads=[lvb], begins=[ltb])
        kb.op(kb.dve, nc.vector.tensor_tensor, out=lt[:, 1, :], in0=lv[:, 128:192], in1=lv[:, 192:256], op=ALU.mult, reads=[lvb], writes=[ltb])
        kb.op(kb.dve, nc.vector.reduce_sum, out=lam[:, 0:2], in_=lt[:], axis=AX.X, reads=[ltb], begins=[lamb])
        kb.op(kb.act, nc.scalar.activation, out=lam[:, 0:2], in_=lam[:, 0:2], func=AF.Exp, reads=[lamb], writes=[lamb])
        kb.op(kb.dve, nc.vector.tensor_tensor, out=lam[:, 2:3], in0=lam[:, 1:2], in1=lam[:, 0:1], op=ALU.subtract, reads=[lamb], writes=[lamb])
        kb.op(kb.dve, nc.vector.tensor_scalar, out=lam[:, 3:4], in0=lam[:, 2:3], scalar1=-LAM_INIT0, scalar2=None, op0=ALU.add,
              reads=[lamb], writes=[lamb])
        sw, swb = sb(kb, es, "subln", [128, 1], F32)
        kb.dma(kb.sp, out=sw[:], in_=dr["diff_subln"].rearrange("(p o) -> p o", o=1), sbuf_buf=swb, begins=[swb])
        kb.op(kb.dve, nc.vector.tensor_scalar, out=sw[:], in0=sw[:], scalar1=(1.0 - LAM_INIT0), scalar2=None, op0=ALU.mult,
              reads=[swb], writes=[swb])
        if len(list(heads)) > 0:
          with contextlib.ExitStack() as es2:
            COS, COSb = sb(kb, es2, "cos", [128, NLAT], F32)
            SIN, SINb = sb(kb, es2, "sin", [128, NLAT], F32)
            kb.dma(kb.sp, out=COS[:], in_=dr["rope_cos"], sbuf_buf=COSb, begins=[COSb])
            kb.dma(kb.sp, out=SIN[:], in_=dr["rope_sin"], sbuf_buf=SINb, begins=[SINb])
            wp = Pool(kb, es2, "wqkv", [128, KC, 3, 128], BF16, 2)
            wsp = Pool(kb, es2, "wsw", [128, KC, 2, 128], BF16, 2)
            qTp = Pool(kb, es2, "qT", [128, T], BF16, 2)
            kTp = Pool(kb, es2, "kT", [128, T], BF16, 2)
            vp = Pool(kb, es2, "vtok", [128, NT, 128], BF16, 2)
            t1p = Pool(kb, es2, "rt1", [128, 512], F32, 2)
            t2p = Pool(kb, es2, "rt2", [128, 512], F32, 2)
            pp = Pool(kb, es2, "pexp", [128, 512], BF16, 3)
            rlp = Pool(kb, es2, "rl", [128, 512], F32, 2)
            op_ = Pool(kb, es2, "om", [128, 2, 512], F32, 2)
            sqp = Pool(kb, es2, "sq", [128, 512], F32, 2)
            yp = Pool(kb, es2, "yat", [128, 512], BF16, 2)
            for h in heads:
                w, wb = wp.next()
                for i, base in enumerate((0, 1536, 3072)):
                    kb.dma(kb.pool, out=w[:, :, i, :], in_=win[:, :, base + h * 128:base + (h + 1) * 128], sbuf_buf=wb,
                           begins=[wb] if i == 0 else (), writes=[wb] if i else ())
                ws, wsb = wsp.next()
                for i in range(2):
                    src = w[:, :, i, :].rearrange("p k (g hh e) -> p k g hh e", g=4, hh=2, e=16)
                    dst = ws[:, :, i, :].rearrange("p k (g hh e) -> p k g hh e", g=4, hh=2, e=16)
                    for hh in range(2):
                        kb.op(kb.dve, nc.vector.tensor_copy, out=dst[:, :, :, hh, :], in_=src[:, :, :, 1 - hh, :], reads=[wb],
                              begins=[wsb] if (i == 0 and hh == 0) else (), writes=() if (i == 0 and hh == 0) else [wsb])
                qT, qTb = qTp.next()
                kT, kTb = kTp.next()
                first = {0: True, 1: True}
                for i, (dst, dstb) in enumerate(((qT, qTb), (kT, kTb))):
                    ps, psb = psA.next()
                    for kc in range(KC):
                        kb.op(kb.pe, nc.tensor.matmul, ps[:, 0:NCTX], lhsT=w[:, kc, i, :], rhs=uT[:, kc, 0:NCTX], start=(kc == 0),
                              stop=(kc == KC - 1), reads=[wb, uTb], begins=[psb] if kc == 0 else (), writes=[psb] if kc else (),
                              signal=(kc == KC - 1))
                    kb.op(kb.act, nc.scalar.activation, out=dst[:, 0:NCTX], in_=ps[:, 0:NCTX], func=AF.Identity, scale=1.0,
                          reads=[psb], begins=[dstb])
                    for qb in range(4):
                        c0 = NCTX + qb * 512
                        ps, psb = psA.next()
                        for kc in range(KC):
                            kb.op(kb.pe, nc.tensor.matmul, ps[:], lhsT=w[:, kc, i, :], rhs=uT[:, kc, c0:c0 + 512], start=(kc == 0),
                                  stop=(kc == KC - 1), reads=[wb, uTb], begins=[psb] if kc == 0 else (), writes=[psb] if kc else (),
                                  signal=(kc == KC - 1))
                        t1, t1b = t1p.next()
                        kb.op(kb.dve, nc.vector.tensor_tensor, out=t1[:], in0=ps[:], in1=COS[:, qb * 512:(qb + 1) * 512], op=ALU.mult,
                              reads=[psb, COSb], begins=[t1b])
                        ps2, ps2b = psA.next()
                        for kc in range(KC):
                            kb.op(kb.pe, nc.tensor.matmul, ps2[:], lhsT=ws[:, kc, i, :], rhs=uT[:, kc, c0:c0 + 512], start=(kc == 0),
                                  stop=(kc == KC - 1), reads=[wsb, uTb], begins=[ps2b] if kc == 0 else (), writes=[ps2b] if kc else (),
                                  signal=(kc == KC - 1))
                        t2, t2b = t2p.next()
                        kb.op(kb.dve, nc.vector.tensor_tensor, out=t2[:], in0=ps2[:], in1=SIN[:, qb * 512:(qb + 1) * 512], op=ALU.mult,
                              reads=[ps2b, SINb], begins=[t2b])
                        kb.op(kb.pool, nc.gpsimd.tensor_tensor, out=dst[:, c0:c0 + 512], in0=t1[:], in1=t2[:], op=ALU.add,
                              reads=[t1b, t2b], writes=[dstb])
                v, vb = vp.next()
                for tt in range(NT):
                    ps, psb = psA.next()
                    for kc in range(KC):
                        kb.op(kb.pe, nc.tensor.matmul, ps[:, 0:128], lhsT=uT[:, kc, tt * 128:(tt + 1) * 128], rhs=w[:, kc, 2, :],
                              start=(kc == 0), stop=(kc == KC - 1), reads=[wb, uTb], begins=[psb] if kc == 0 else (),
                              writes=[psb] if kc else (), signal=(kc == KC - 1))
                    kb.op(kb.act, nc.scalar.activation, out=v[:, tt, :], in_=ps[:, 0:128], func=AF.Identity, scale=1.0, reads=[psb],
                          begins=[vb] if tt == 0 else (), writes=[vb] if tt else ())
                blocks = [(0, NCTX, [0, 1], 0)] + [(NCTX + qb * 512, 512, list(range(NT)), NCTX + qb * 512) for qb in range(4)]
                for (q0, nq, kts, cc0) in blocks:
                    om, omb = op_.next()
                    for m in range(2):
                        r0 = m * 64
                        po, pob = psO.next()
                        pl, plb = psL.next()
                        def score(kt):
                            ps, psb = psS.next()
                            kb.op(kb.pe, nc.tensor.matmul, ps[:, 0:nq], lhsT=kT[r0:r0 + 64, kt * 128:(kt + 1) * 128],
                                  rhs=qT[r0:r0 + 64, q0:q0 + nq], start=True, stop=True, reads=[kTb, qTb], begins=[psb])
                            return ps, psb
                        nxt = score(kts[0])
                        for ki, kt in enumerate(kts):
                            ps, psb = nxt
                            if ki + 1 < len(kts):
                                nxt = score(kts[ki + 1])
                            p, pb = pp.next()
                            kb.op(kb.act, nc.scalar.activation, out=p[:, 0:nq], in_=ps[:, 0:nq], func=AF.Exp, scale=0.125, reads=[psb],
                                  begins=[pb])
                            kb.op(kb.pe, nc.tensor.matmul, po[:, 0:nq], lhsT=v[:, kt, :], rhs=p[:, 0:nq], start=(ki == 0),
                                  stop=(ki == len(kts) - 1), reads=[vb, pb], begins=[pob] if ki == 0 else (), writes=[pob] if ki else (),
                                  signal=False)
                            kb.op(kb.pe, nc.tensor.matmul, pl[:, 0:nq], lhsT=cs.onesb[:], rhs=p[:, 0:nq], start=(ki == 0),
                                  stop=(ki == len(kts) - 1), reads=[cs.onesb_b, pb], begins=[plb] if ki == 0 else (),
                                  writes=[plb] if ki else (), signal=True)
                        rl, rlb = rlp.next()
                        kb.op(kb.dve, nc.vector.reciprocal, out=rl[:, 0:nq], in_=pl[:, 0:nq], reads=[plb], begins=[rlb])
                        kb.op(kb.dve, nc.vector.tensor_tensor, out=om[:, m, 0:nq], in0=po[:, 0:nq], in1=rl[:, 0:nq], op=ALU.mult,
                              reads=[pob, rlb], begins=[omb] if m == 0 else (), writes=[omb] if m else ())
                    kb.op(kb.dve, nc.vector.scalar_tensor_tensor, out=om[:, 0, 0:nq], in0=om[:, 1, 0:nq], scalar=lam[:, 3:4],
                          in1=om[:, 0, 0:nq], op0=ALU.mult, op1=ALU.add, reads=[omb, lamb], writes=[omb])
                    sq, sqb = sqp.next()
                    kb.op(kb.act, nc.scalar.activation, out=sq[:, 0:nq], in_=om[:, 0, 0:nq], func=AF.Square, reads=[omb], begins=[sqb])
                    pm, pmb = psS.next()
                    kb.op(kb.pe, nc.tensor.matmul, pm[:, 0:nq], lhsT=cs.onesf[:], rhs=sq[:, 0:nq], start=True, stop=True,
                          reads=[cs.onesf_b, sqb], begins=[pmb])
                    kb.op(kb.act, nc.scalar.activation, out=sq[:, 0:nq], in_=pm[:, 0:nq], func=AF.Sqrt, scale=1.0 / 128, bias=cs.epsr[:],
                          reads=[pmb, cs.epsr_b], writes=[sqb])
                    kb.op(kb.dve, nc.vector.reciprocal, out=sq[:, 0:nq], in_=sq[:, 0:nq], reads=[sqb], writes=[sqb])
                    kb.op(kb.dve, nc.vector.tensor_tensor, out=sq[:, 0:nq], in0=sq[:, 0:nq], in1=om[:, 0, 0:nq], op=ALU.mult,
                          reads=[sqb, omb], writes=[sqb])
                    y, yb = yp.next()
                    kb.op(kb.act, nc.scalar.activation, out=y[:, 0:nq], in_=sq[:, 0:nq], func=AF.Identity, scale=sw[:, 0:1],
                          reads=[sqb, swb], begins=[yb])
                    kb.dma(kb.sp, out=cat[h * 128:(h + 1) * 128, cc0:cc0 + nq], in_=y[:, 0:nq], sbuf_buf=yb, reads=[yb])
            kb.barrier()
        if do_fourier:
          with contextlib.ExitStack() as es2:
            cs128, cs128b = sb(kb, es2, "cs128", [128, 256], BF16)
            kb.dma(kb.sp, out=cs128[:], in_=dr["dft128"], sbuf_buf=cs128b, begins=[cs128b])
            d256, d256b = sb(kb, es2, "d256", [128, 2, 2, 256], BF16)
            for tb_ in range(2):
                kb.dma(kb.sp, out=d256[:, tb_, :, :], in_=dr["dft256"][tb_].rearrange("(tt p) n -> p tt n", p=128), sbuf_buf=d256b,
                       begins=[d256b] if tb_ == 0 else (), writes=[d256b] if tb_ else ())
            AB, ABb = sb(kb, es2, "AB", [128, 4, NT, 256], BF16)
            wfp = Pool(kb, es2, "wf", [128, KC, 128], BF16, 2)
            fTp = Pool(kb, es2, "fT", [128, T], BF16, 2)
            for g in range(4):
                wf, wfb = wfp.next()
                kb.dma(kb.pool, out=wf[:], in_=win[:, :, 4608 + g * 128:4608 + (g + 1) * 128], sbuf_buf=wfb, begins=[wfb])
                fT, fTb = fTp.next()
                for bi, (c0, n) in enumerate([(0, 256)] + [(NCTX + i * 512, 512) for i in range(4)]):
                    ps, psb = psA.next()
                    for kc in range(KC):
                        kb.op(kb.pe, nc.tensor.matmul, ps[:, 0:n], lhsT=wf[:, kc, :], rhs=uT[:, kc, c0:c0 + n], start=(kc == 0),
                              stop=(kc == KC - 1), reads=[wfb, uTb], begins=[psb] if kc == 0 else (), writes=[psb] if kc else (),
                              signal=(kc == KC - 1))
                    kb.op(kb.act, nc.scalar.activation, out=fT[:, c0:c0 + n], in_=ps[:, 0:n], func=AF.Identity, scale=1.0, reads=[psb],
                          begins=[fTb] if bi == 0 else (), writes=[fTb] if bi else ())
                for tt in range(NT):
                    ps, psb = psS.next()
                    kb.op(kb.pe, nc.tensor.matmul, ps[:, 0:256], lhsT=fT[:, tt * 128:(tt + 1) * 128], rhs=cs128[:], start=True, stop=True,
                          reads=[fTb, cs128b], begins=[psb])
                    first = (g == 0 and tt == 0)
                    kb.op(kb.dve, nc.vector.tensor_copy, out=AB[:, g, tt, :], in_=ps[:, 0:256], reads=[psb],
                          begins=[ABb] if first else (), writes=() if first else [ABb])
            yfp = Pool(kb, es2, "yf", [128, 512], BF16, 2)
            for g in range(4):
                ps, psb = psA.next()
                k = 0
                for tt in range(2):
                    for tb_ in range(2):
                        kb.op(kb.pe, nc.tensor.matmul, ps[:, 0:256], lhsT=AB[:, g, tt, tb_ * 128:(tb_ + 1) * 128], rhs=d256[:, tb_, tt, :],
                              start=(k == 0), stop=(k == 3), reads=[ABb, d256b], begins=[psb] if k == 0 else (), writes=[psb] if k else (),
                              signal=(k == 3))
                        k += 1
                yf, yfb = yfp.next()
                kb.op(kb.act, nc.scalar.activation, out=yf[:, 0:256], in_=ps[:, 0:256], func=AF.Identity, scale=(256 * 128) ** -0.5,
                      reads=[psb], begins=[yfb])
                kb.dma(kb.sp, out=cat[1536 + g * 128:1536 + (g + 1) * 128, 0:NCTX], in_=yf[:, 0:256], sbuf_buf=yfb, reads=[yfb])
            slp = Pool(kb, es2, "dslab", [128, 2, 16, 512], BF16, 2)
            for pb_ in range(4):
                sl, slb = slp.next()
                for tb_ in range(2):
                    kb.dma(kb.sp, out=sl[:, tb_, :, :], in_=dr["dft2048"][tb_, :, pb_ * 512:(pb_ + 1) * 512].rearrange("(tt p) n -> p tt n", p=128),
                           sbuf_buf=slb, begins=[slb] if tb_ == 0 else (), writes=[slb] if tb_ else ())
                for g in range(4):
                    ps, psb = psA.next()
                    k = 0
                    for tt in range(16):
                        for tb_ in range(2):
                            kb.op(kb.pe, nc.tensor.matmul, ps[:], lhsT=AB[:, g, 2 + tt, tb_ * 128:(tb_ + 1) * 128], rhs=sl[:, tb_, tt, :],
                                  start=(k == 0), stop=(k == 31), reads=[ABb, slb], begins=[psb] if k == 0 else (),
                                  writes=[psb] if k else (), signal=(k == 31))
                            k += 1
                    yf, yfb = yfp.next()
                    kb.op(kb.act, nc.scalar.activation, out=yf[:], in_=ps[:], func=AF.Identity, scale=(2048 * 128) ** -0.5, reads=[psb],
                          begins=[yfb])
                    kb.dma(kb.sp, out=cat[1536 + g * 128:1536 + (g + 1) * 128, NCTX + pb_ * 512:NCTX + (pb_ + 1) * 512], in_=yf[:],
                           sbuf_buf=yfb, reads=[yfb])
            kb.barrier()
        kb.end_phase()


def host_tables():
    inv = 1.0 / (10000.0 ** (np.arange(0, 32, 2, dtype=np.float32) / 32))
    tok = np.arange(NLAT)
    rows, cols = tok // 64, tok % 64
    ang_r = rows[None, :].astype(np.float32) * inv[:, None]
    ang_c = cols[None, :].astype(np.float32) * inv[:, None]
    cos64 = np.concatenate([np.cos(ang_r), np.cos(ang_r), np.cos(ang_c), np.cos(ang_c)], 0)
    sin64 = np.concatenate([-np.sin(ang_r), np.sin(ang_r), -np.sin(ang_c), np.sin(ang_c)], 0)
    cos = np.concatenate([cos64, cos64], 0).astype(np.float32)
    sin = np.concatenate([sin64, sin64], 0).astype(np.float32)

    def dft(n):
        i = np.arange(n, dtype=np.int64)
        a = 2.0 * np.pi * ((i[:, None] * i[None, :]) % n).astype(np.float64) / n
        return np.cos(a), np.sin(a)
    c128, s128 = dft(128)
    d128 = np.concatenate([c128, -s128], 1).astype(ml_dtypes.bfloat16)
    c256, s256 = dft(256)
    d256 = np.stack([c256, s256]).astype(ml_dtypes.bfloat16)
    c2k, s2k = dft(2048)
    d2k = np.stack([c2k, s2k]).astype(ml_dtypes.bfloat16)
    return {"rope_cos": cos, "rope_sin": sin, "dft128": d128, "dft256": d256, "dft2048": d2k}


NCH = T // 64


def phase_hgrn(kb, dr, heads=range(16)):
    nc = kb.nc
    win = dr["hgrn_w_in"].rearrange("(kc p) n -> p kc n", p=128)
    cat = dr["cat1"]
    blocks5 = [(0, 256)] + [(NCTX + i * 512, 512) for i in range(4)]
    with contextlib.ExitStack() as es:
        kb.start_phase()
        cs = make_consts(kb, es)
        lr = LNRes(kb, es)
        psA = Pool(kb, es, "psA", [128, 512], F32, 2, psum=True)
        psS = Pool(kb, es, "psS", [128, 512], F32, 2, psum=True)
        psO = Pool(kb, es, "psO", [128, 512], F32, 1, psum=True)
        psK = Pool(kb, es, "psK", [128, 512], F32, 1, psum=True)
        pst = Pool(kb, es, "pst", [128, 1024], BF16, 2, psum=True)
        mods = load_modT(kb, es, cs, dr, 1, "modT", psA)
        uT, uTb = sb(kb, es, "uT", [128, KC, T], BF16)
        with contextlib.ExitStack() as es2:
            xp = Pool(kb, es2, "xin", [128, D], F32, 2)
            nbp = Pool(kb, es2, "nb", [128, D], BF16, 2)
            for tt in range(NT):
                r = 0 if tt >= 2 else 1
                x, xb = xp.next()
                kb.dma(kb.sp, out=x[:], in_=dr["hres"][tt * 128:(tt + 1) * 128, :], sbuf_buf=xb, begins=[xb])
                emit_ln_to_uT(kb, cs, lr, x, xb, nbp, pst, mods[r][0][:, 16:32], mods[r][0][:, 0:16], mods[r][1], uT, uTb,
                              tt * 128, tt == 0)
            kb.barrier()
        lbs = []
        for d in range(2):
            b0, b0b = load_colT(kb, es, cs, f"lb0{d}", dr["hgrn_lower_bounds"][d, 0], 16, psA)
            b1, b1b = load_colT(kb, es, cs, f"lb1{d}", dr["hgrn_lower_bounds"][d, 1], 16, psA)
            kb.op(kb.dve, nc.vector.tensor_tensor, out=b1[:], in0=b1[:], in1=b0[:], op=ALU.subtract, reads=[b0b, b1b], writes=[b1b])
            kb.op(kb.act, nc.scalar.activation, out=b1[:], in_=b1[:], func=AF.Sigmoid, reads=[b1b], writes=[b1b])
            kb.op(kb.dve, nc.vector.tensor_scalar, out=b0[:], in0=b1[:], scalar1=-1.0, scalar2=1.0, op0=ALU.mult, op1=ALU.add,
                  reads=[b1b], writes=[b0b])
            nml, nmlb = sb(kb, es, f"noml{d}", [128, 16], F32)
            kb.op(kb.dve, nc.vector.tensor_scalar, out=nml[:], in0=b0[:], scalar1=-1.0, scalar2=None, op0=ALU.mult, reads=[b0b], begins=[nmlb])
            lbs.append((b1, b1b, b0, b0b, nml, nmlb))
        NW, NWb = load_bcast(kb, es, "hnw", dr["hgrn_norm"], 128)
        m01, m01b = sb(kb, es, "m01", [128, T], F32)
        kb.op(kb.pool, nc.gpsimd.memset, m01[:], 1.0, begins=[m01b])
        kb.op(kb.pool, nc.gpsimd.memset, m01[:].rearrange("p (c s) -> p c s", s=64)[:, :, 0:1], 0.0, reads=[m01b], writes=[m01b])
        masks = []
        for d in range(2):
            mk, mkb = sb(kb, es, f"tri{d}", [128, 64], F32)
            kb.op(kb.pool, nc.gpsimd.memset, mk[:], 1.0, begins=[mkb])
            for r0 in (0, 64):
                kb.op(kb.pool, nc.gpsimd.affine_select, out=mk[r0:r0 + 64, :], in_=mk[r0:r0 + 64, :],
                      pattern=[[1 if d == 0 else -1, 64]], compare_op=ALU.is_ge, fill=0.0, base=0,
                      channel_multiplier=(-1 if d == 0 else 1), reads=[mkb], writes=[mkb])
            masks.append((mk, mkb))
        w5, w5b = sb(kb, es, "w5", [128, KC, 5, 128], BF16)
        qsil, qsilb = sb(kb, es, "qsil", [128, T], F32)
        A, Ab = sb(kb, es, "hA", [128, T], F32)
        Bk, Bkb = sb(kb, es, "hB", [128, T], F32)
        C, Cb = sb(kb, es, "hC", [128, T], F32)
        qt, qtb = sb(kb, es, "hqt", [128, T], BF16)
        kt, ktb = sb(kb, es, "hkt", [128, T], BF16)
        kh, khb = sb(kb, es, "hkh", [128, T], BF16)
        ktok, ktokb = sb(kb, es, "hktok", [128, NT, 128], BF16)
        v, vb = sb(kb, es, "hv", [128, NT, 128], BF16)
        gsil, gsilb = sb(kb, es, "hg", [128, 16, 128], F32)
        oacc, oaccb = sb(kb, es, "hoacc", [128, 16, 128], F32)
        ref, refb = sb(kb, es, "href", [128, NCH], F32)
        cend, cendb = sb(kb, es, "hcend", [128, NCH], F32)
        gam, gamb = sb(kb, es, "hgam", [128, NCH], F32)
        tot, totb = sb(kb, es, "htot", [128, NCH], F32)
        Sm, Smb_ = sb(kb, es, "hSm", [128, 128], F32)
        Sbp = Pool(kb, es, "hSb", [128, 128], BF16, 2)
        scp = Pool(kb, es, "hsc", [128, 64], BF16, 2)
        ssq, ssqb = sb(kb, es, "hssq", [128, 16], F32)
        junk, junkb = sb(kb, es, "hjunk", [128, 128], F32)
        ytp = Pool(kb, es, "hyt", [128, 128], BF16, 2)
        yTp = Pool(kb, es, "hyT", [128, 512], BF16, 2)
        secs = (0, 2048, 4096, 6144, 8192)
        v3 = lambda t: t[:].rearrange("p (c s) -> p c s", s=64)
        for h in heads:
            for i, base in enumerate(secs):
                kb.dma(kb.pool, out=w5[:, :, i, :], in_=win[:, :, base + h * 128:base + (h + 1) * 128], sbuf_buf=w5b,
                       begins=[w5b] if i == 0 else (), writes=[w5b] if i else ())
            for tt in range(NT):
                ps, psb = psA.next()
                for kc in range(KC):
                    kb.op(kb.pe, nc.tensor.matmul, ps[:, 0:256], lhsT=uT[:, kc, tt * 128:(tt + 1) * 128], rhs=w5[:, kc, 3:5, :],
                          start=(kc == 0), stop=(kc == KC - 1), reads=[w5b, uTb], begins=[psb] if kc == 0 else (),
                          writes=[psb] if kc else (), signal=(kc == KC - 1))
                kb.op(kb.act, nc.scalar.activation, out=v[:, tt, :], in_=ps[:, 0:128], func=AF.Identity, scale=1.0, reads=[psb],
                      begins=[vb] if tt == 0 else (), writes=[vb] if tt else ())
                if tt >= 2:
                    kb.op(kb.act, nc.scalar.activation, out=gsil[:, tt - 2, :], in_=ps[:, 128:256], func=AF.Silu, reads=[psb],
                          begins=[gsilb] if tt == 2 else (), writes=[gsilb] if tt > 2 else ())
            for bi, (c0, n) in enumerate(blocks5):
                ps, psb = psA.next()
                for kc in range(KC):
                    kb.op(kb.pe, nc.tensor.matmul, ps[:, 0:n], lhsT=w5[:, kc, 0, :], rhs=uT[:, kc, c0:c0 + n], start=(kc == 0),
                          stop=(kc == KC - 1), reads=[w5b, uTb], begins=[psb] if kc == 0 else (), writes=[psb] if kc else (),
                          signal=(kc == KC - 1))
                kb.op(kb.act, nc.scalar.activation, out=qsil[:, c0:c0 + n], in_=ps[:, 0:n], func=AF.Silu, reads=[psb],
                      begins=[qsilb] if bi == 0 else (), writes=[qsilb] if bi else ())
            for d in range(2):
                lb, lbb, oml, omlb, nml, nmlb = lbs[d]
                mk, mkb = masks[d]
                for bi, (c0, n) in enumerate(blocks5):
                    ps, psb = psA.next()
                    for kc in range(KC):
                        kb.op(kb.pe, nc.tensor.matmul, ps[:, 0:n], lhsT=w5[:, kc, 1 + d, :], rhs=uT[:, kc, c0:c0 + n], start=(kc == 0),
                              stop=(kc == KC - 1), reads=[w5b, uTb], begins=[psb] if kc == 0 else (), writes=[psb] if kc else (),
                              signal=(kc == KC - 1))
                    kb.op(kb.act, nc.scalar.activation, out=A[:, c0:c0 + n], in_=ps[:, 0:n], func=AF.Sigmoid, reads=[psb],
                          begins=[Ab] if bi == 0 else (), writes=[Ab] if bi else ())
                kb.op(kb.dve, nc.vector.tensor_scalar, out=Bk[:], in0=A[:], scalar1=nml[:, h:h + 1], scalar2=oml[:, h:h + 1],
                      op0=ALU.mult, op1=ALU.add, reads=[Ab, nmlb, omlb], begins=[Bkb])
                kb.op(kb.dve, nc.vector.tensor_scalar, out=A[:], in0=A[:], scalar1=oml[:, h:h + 1], scalar2=lb[:, h:h + 1],
                      op0=ALU.mult, op1=ALU.add, reads=[Ab, omlb, lbb], writes=[Ab])
                kb.op(kb.act, nc.scalar.activation, out=A[:], in_=A[:], func=AF.Ln, reads=[Ab], writes=[Ab])
                kb.op(kb.dve, nc.vector.tensor_tensor_scan, out=C[:], data0=m01[:], data1=A[:], initial=0.0, op0=ALU.mult, op1=ALU.add,
                      reads=[m01b, Ab], begins=[Cb])
                if d == 1:
                    kb.op(kb.dve, nc.vector.tensor_copy, out=tot[:], in_=v3(C)[:, :, 63], reads=[Cb], begins=[totb])
                    kb.op(kb.dve, nc.vector.tensor_tensor, out=C[:], in0=A[:], in1=C[:], op=ALU.subtract, reads=[Ab, Cb], writes=[Cb])
                    kb.op(kb.dve, nc.vector.tensor_tensor, out=v3(C), in0=v3(C), in1=tot[:].unsqueeze(2).to_broadcast([128, NCH, 64]),
                          op=ALU.add, reads=[Cb, totb], writes=[Cb])
                mid = 31 if d == 0 else 32
                endi = 63 if d == 0 else 0
                kb.op(kb.dve, nc.vector.tensor_copy, out=ref[:], in_=v3(C)[:, :, mid], reads=[Cb], begins=[refb])
                kb.op(kb.dve, nc.vector.tensor_copy, out=cend[:], in_=v3(C)[:, :, endi], reads=[Cb], begins=[cendb])
                kb.op(kb.dve, nc.vector.tensor_tensor, out=v3(C), in0=v3(C), in1=ref[:].unsqueeze(2).to_broadcast([128, NCH, 64]),
                      op=ALU.subtract, reads=[Cb, refb], writes=[Cb])
                kb.op(kb.act, nc.scalar.activation, out=A[:], in_=C[:], func=AF.Exp, reads=[Cb], writes=[Ab])
                kb.op(kb.dve, nc.vector.tensor_tensor, out=qt[:], in0=qsil[:], in1=A[:], op=ALU.mult, reads=[qsilb, Ab], begins=[qtb])
                kb.op(kb.act, nc.scalar.activation, out=A[:], in_=C[:], func=AF.Exp, scale=-1.0, reads=[Cb, qtb], writes=[Ab])
                kb.op(kb.dve, nc.vector.tensor_tensor, out=kt[:], in0=Bk[:], in1=A[:], op=ALU.mult, reads=[Bkb, Ab], begins=[ktb])
                order = list(range(NCH)) if d == 0 else [3, 2, 1, 0] + list(range(NCH - 1, 3, -1))
                kb.op(kb.dve, nc.vector.tensor_tensor, out=gam[:], in0=cend[:], in1=ref[:], op=ALU.subtract, reads=[cendb, refb], begins=[gamb])
                if d == 0:
                    kb.op(kb.dve, nc.vector.tensor_tensor, out=gam[:, 0:NCH - 1], in0=gam[:, 0:NCH - 1], in1=ref[:, 1:NCH], op=ALU.add,
                          reads=[gamb, refb], writes=[gamb])
                else:
                    kb.op(kb.dve, nc.vector.tensor_tensor, out=gam[:, 1:NCH], in0=gam[:, 1:NCH], in1=ref[:, 0:NCH - 1], op=ALU.add,
                          reads=[gamb, refb], writes=[gamb])
                    kb.op(kb.dve, nc.vector.tensor_tensor, out=gam[:, 0:1], in0=gam[:, 0:1], in1=ref[:, NCH - 1:NCH], op=ALU.add,
                          reads=[gamb, refb], writes=[gamb])
                kb.op(kb.act, nc.scalar.activation, out=gam[:], in_=gam[:], func=AF.Exp, reads=[gamb], writes=[gamb])
                kb.op(kb.dve, nc.vector.tensor_tensor, out=v3(kh), in0=v3(kt), in1=gam[:].unsqueeze(2).to_broadcast([128, NCH, 64]),
                      op=ALU.mult, reads=[ktb, gamb], begins=[khb])
                for q4 in range(0, NT, 4):
                    pt, ptb = pst.next()
                    nn = min(4, NT - q4)
                    for j in range(nn):
                        tt = q4 + j
                        kb.op(kb.pe, nc.tensor.transpose, out=pt[:, j * 128:(j + 1) * 128], in_=kh[:, tt * 128:(tt + 1) * 128],
                              identity=cs.identb[:], reads=[khb, cs.identb_b], begins=[ptb] if j == 0 else (),
                              writes=[ptb] if j else (), signal=(j == nn - 1))
                    kb.op(kb.act, nc.scalar.activation, out=ktok[:, q4:q4 + nn, :], in_=pt[:, 0:nn * 128].rearrange("p (a b) -> p a b", b=128),
                          func=AF.Identity, scale=1.0, reads=[ptb], begins=[ktokb] if q4 == 0 else (), writes=[ktokb] if q4 else ())
                kb.op(kb.dve, nc.vector.memset, Sm[:], 0.0, begins=[Smb_])
                for c in order:
                    tt, r0 = c // 2, (c % 2) * 64
                    c0 = c * 64
                    lat = c >= 4
                    if lat:
                        Sb, Sbb = Sbp.next()
                        kb.op(kb.act, nc.scalar.activation, out=Sb[:], in_=Sm[:], func=AF.Identity, scale=1.0, reads=[Smb_], begins=[Sbb])
                        ps, psb = psS.next()
                        kb.op(kb.pe, nc.tensor.matmul, ps[r0:r0 + 64, 0:64], lhsT=kt[:, c0:c0 + 64], rhs=qt[:, c0:c0 + 64], start=True,
                              stop=True, reads=[ktb, qtb], begins=[psb])
                        sc, scb = scp.next()
                        kb.op(kb.dve, nc.vector.tensor_tensor, out=sc[r0:r0 + 64, :], in0=ps[r0:r0 + 64, 0:64], in1=mk[r0:r0 + 64, :],
                              op=ALU.mult, reads=[psb, mkb], begins=[scb])
                        po, pob = psO.next()
                        kb.op(kb.pe, nc.tensor.matmul, po[r0:r0 + 64, 0:128], lhsT=qt[:, c0:c0 + 64], rhs=Sb[:], start=True, stop=False,
                              reads=[qtb, Sbb], begins=[pob], signal=False)
                        kb.op(kb.pe, nc.tensor.matmul, po[r0:r0 + 64, 0:128], lhsT=sc[r0:r0 + 64, :], rhs=v[r0:r0 + 64, tt, :], start=False,
                              stop=True, reads=[scb, vb], writes=[pob])
                        if d == 0:
                            kb.op(kb.dve, nc.vector.tensor_copy, out=oacc[r0:r0 + 64, tt - 2, :], in_=po[r0:r0 + 64, 0:128], reads=[pob],
                                  begins=[oaccb] if c == 4 else (), writes=[oaccb] if c != 4 else ())
                        else:
                            kb.op(kb.dve, nc.vector.tensor_tensor, out=oacc[r0:r0 + 64, tt - 2, :], in0=oacc[r0:r0 + 64, tt - 2, :],
                                  in1=po[r0:r0 + 64, 0:128], op=ALU.add, reads=[pob, oaccb], writes=[oaccb])
                    pk, pkb = psK.next()
                    kb.op(kb.pe, nc.tensor.matmul, pk[:, 0:128], lhsT=ktok[r0:r0 + 64, tt, :], rhs=v[r0:r0 + 64, tt, :], start=True, stop=True,
                          reads=[ktokb, vb], begins=[pkb])
                    kb.op(kb.dve, nc.vector.scalar_tensor_tensor, out=Sm[:], in0=Sm[:], scalar=gam[:, c:c + 1], in1=pk[:, 0:128],
                          op0=ALU.mult, op1=ALU.add, reads=[Smb_, gamb, pkb], writes=[Smb_])
            for q4 in range(0, 16, 4):
                pt, ptb = pst.next()
                for j in range(4):
                    tl = q4 + j
                    kb.op(kb.act, nc.scalar.activation, out=junk[:], in_=oacc[:, tl, :], func=AF.Square, accum_out=ssq[:, tl:tl + 1],
                          reads=[oaccb], begins=[junkb, ssqb] if tl == 0 else [junkb], writes=[ssqb] if tl else ())
                    kb.op(kb.act, nc.scalar.activation, out=ssq[:, tl:tl + 1], in_=ssq[:, tl:tl + 1], func=AF.Sqrt, scale=1.0 / 128,
                          bias=cs.epsr[:], reads=[ssqb, cs.epsr_b], writes=[ssqb])
                    kb.op(kb.dve, nc.vector.reciprocal, out=ssq[:, tl:tl + 1], in_=ssq[:, tl:tl + 1], reads=[ssqb], writes=[ssqb])
                    kb.op(kb.dve, nc.vector.scalar_tensor_tensor, out=junk[:], in0=oacc[:, tl, :], scalar=ssq[:, tl:tl + 1], in1=NW[:],
                          op0=ALU.mult, op1=ALU.mult, reads=[oaccb, ssqb, NWb], writes=[junkb])
                    yt, ytb = ytp.next()
                    kb.op(kb.dve, nc.vector.tensor_tensor, out=yt[:], in0=junk[:], in1=gsil[:, tl, :], op=ALU.mult, reads=[junkb, gsilb],
                          begins=[ytb])
                    kb.op(kb.pe, nc.tensor.transpose, out=pt[:, j * 128:(j + 1) * 128], in_=yt[:], identity=cs.identb[:],
                          reads=[ytb, cs.identb_b], begins=[ptb] if j == 0 else (), writes=[ptb] if j else (), signal=True)
                yT, yTb = yTp.next()
                kb.op(kb.act, nc.scalar.activation, out=yT[:], in_=pt[:, 0:512], func=AF.Identity, scale=1.0, reads=[ptb], begins=[yTb])
                kb.dma(kb.sp, out=cat[h * 128:(h + 1) * 128, NCTX + q4 * 128:NCTX + (q4 + 4) * 128], in_=yT[:], sbuf_buf=yTb, reads=[yTb])
        kb.end_phase()


BPC = 1


def build_program(bpc=BPC):
    nc = bass.Bass("TRN2", target_bir_lowering=False)
    dr = {}

    def din(name, shape, dt=F32):
        dr[name] = nc.dram_tensor(name, shape, dt, kind="ExternalInput").ap()

    def dint(name, shape, dt=F32):
        dr[name] = nc.dram_tensor(name, shape, dt, kind="Internal").ap()

    din("x", [bpc, NLAT, D]); din("ctx", [bpc, NCTX, D]); din("c2", [bpc, 2, D])
    din("mod_w", [2, D, 6 * D]); din("mod_b", [2, 6 * D])
    din("ln_mix_g", [2, D]); din("ln_mix_b", [2, D]); din("ln_ffn_g", [2, D]); din("ln_ffn_b", [2, D])
    din("even_w_in", [D, 5120]); din("even_w_out", [D, D]); din("diff_lambda", [4, 64]); din("diff_subln", [128])
    din("hgrn_w_in", [D, 10240]); din("hgrn_w_out", [D, D]); din("hgrn_lower_bounds", [2, 2, D]); din("hgrn_norm", [128])
    din("ffn_w_up", [2, D, 2 * DFF]); din("ffn_conv_w", [2, 3, DFF]); din("ffn_conv_b", [2, DFF]); din("ffn_w_down", [2, DFF, D])
    din("rope_cos", [128, NLAT]); din("rope_sin", [128, NLAT]); din("dft128", [128, 256], BF16)
    din("dft256", [2, 256, 256], BF16); din("dft2048", [2, 2048, 2048], BF16)
    yout = nc.dram_tensor("y", [bpc, NLAT, D], F32, kind="ExternalOutput").ap()
    dint("modrow", [2, 2, 6 * D]); dint("cat0", [D, T], BF16); dint("cat1", [D, T], BF16)
    dint("hmid", [T, D]); dint("uf", [128, KC, UFW], BF16); dint("hres", [T, D])
    kb = KB(nc)
    all_tiles = list(range(NT))
    lat_tiles = list(range(2, NT))
    xs, ctxs, c2s = dr["x"], dr["ctx"], dr["c2"]

    with kb.stack:
        for b in range(bpc):
            dr["x"], dr["ctx"], dr["c2"] = xs[b], ctxs[b], c2s[b]
            kb.begin_batch()

            def hin0(tt, b=b):
                return ctxs[b][tt * 128:(tt + 1) * 128, :] if tt < 2 else xs[b][(tt - 2) * 128:(tt - 1) * 128, :]

            def hres_t(tt):
                return dr["hres"][tt * 128:(tt + 1) * 128, :]

            phase_mod(kb, dr)
            phase_even(kb, dr)
            phase_post(kb, dr, 0, dr["cat0"], dr["even_w_out"], hin0, all_tiles, dr["ln_mix_g"][0], dr["ln_mix_b"][0])
            phase_ffn(kb, dr, 0, all_tiles, hres_t, dr["ln_ffn_g"][0], dr["ln_ffn_b"][0])
            phase_hgrn(kb, dr)
            phase_post(kb, dr, 1, dr["cat1"], dr["hgrn_w_out"], hres_t, lat_tiles, dr["ln_mix_g"][1], dr["ln_mix_b"][1])
            phase_ffn(kb, dr, 1, lat_tiles, lambda tt, b=b: yout[b][(tt - 2) * 128:(tt - 1) * 128, :], dr["ln_ffn_g"][1], dr["ln_ffn_b"][1])
    print("semaphores used:", kb.nsem)
    return nc


def kernel(x, c, ctx, c_ctx, mod_w, mod_b, ln_mix_g, ln_mix_b, ln_ffn_g, ln_ffn_b, even_w_in, even_w_out, diff_lambda,
           diff_subln, hgrn_w_in, hgrn_w_out, hgrn_lower_bounds, hgrn_norm, ffn_w_up, ffn_conv_w, ffn_conv_b, ffn_w_down):
    f = lambda a: np.ascontiguousarray(np.asarray(a, dtype=np.float32))
    tabs = host_tables()
    shared = {
        "mod_w": f(mod_w), "mod_b": f(mod_b), "ln_mix_g": f(ln_mix_g), "ln_mix_b": f(ln_mix_b), "ln_ffn_g": f(ln_ffn_g),
        "ln_ffn_b": f(ln_ffn_b), "even_w_in": f(even_w_in)[0], "even_w_out": f(even_w_out)[0], "diff_lambda": f(diff_lambda)[0],
        "diff_subln": f(diff_subln)[0], "hgrn_w_in": f(hgrn_w_in)[0], "hgrn_w_out": f(hgrn_w_out)[0],
        "hgrn_lower_bounds": f(hgrn_lower_bounds), "hgrn_norm": f(hgrn_norm)[0], "ffn_w_up": f(ffn_w_up), "ffn_conv_w": f(ffn_conv_w),
        "ffn_conv_b": f(ffn_conv_b), "ffn_w_down": f(ffn_w_down), **tabs,
    }
    x = f(x); ctx = f(ctx); c = f(c); c_ctx = f(c_ctx)
    nb = x.shape[0]
    ncores = nb // BPC
    in_maps = []
    for i in range(ncores):
        m = dict(shared)
        bs = list(range(i * BPC, (i + 1) * BPC))
        m["x"] = np.ascontiguousarray(x[bs])
        m["ctx"] = np.ascontiguousarray(ctx[bs])
        m["c2"] = np.ascontiguousarray(np.stack([np.stack([c[b], c_ctx]) for b in bs]))
        in_maps.append(m)
    nc = build_program()
    res = run_bass_kernel_spmd(nc, in_maps, core_ids=list(range(ncores)))
    return np.concatenate([np.asarray(r["y"], dtype=np.float32) for r in res.results], axis=0)
```
